# Optimizing a Trainium2 kernel written in Bass

```python
import math
import jax, jax.numpy as jnp
from jax import lax
import numpy as np

D_MODEL = 1024
BATCH = 4
SEQ = 8192
DEPTH = 1

D_MIX = D_MODEL
CONV_CH = D_MIX // 2
CONV_GROUPS = 8
CONV_WIDTH = 31
SB_HEADS = 8
SB_HEAD_DIM = 64
SB_WIDTH = SB_HEADS * SB_HEAD_DIM
Q_BLOCK = 128
D_FF = int(math.ceil((8 * D_MODEL / 3) / 256) * 256)
IN_COLS = 2 * CONV_CH + 3 * SB_WIDTH
EPS = 1e-6

kernel_name = "hymba_conformer_stickbreaking_sandwich"


def _rmsnorm(x, g):
    xf = x.astype(jnp.float32)
    y = xf * lax.rsqrt(jnp.mean(xf * xf, axis=-1, keepdims=True) + EPS)
    return (y * g.astype(jnp.float32)).astype(x.dtype)


def _layernorm(x, g, b):
    xf = x.astype(jnp.float32)
    mu = jnp.mean(xf, axis=-1, keepdims=True)
    var = jnp.mean(jnp.square(xf - mu), axis=-1, keepdims=True)
    y = (xf - mu) * lax.rsqrt(var + EPS)
    return (y * g.astype(jnp.float32) + b.astype(jnp.float32)).astype(x.dtype)


def _conformer_conv(val, gate, conv_w, conv_b, ln_g, ln_b):
    glu = val * jax.nn.sigmoid(gate)
    padded = jnp.pad(glu, ((0, 0), (CONV_WIDTH - 1, 0), (0, 0)))
    y = lax.conv_general_dilated(
        padded, conv_w.astype(glu.dtype), window_strides=(1,), padding="VALID",
        dimension_numbers=("NWC", "WIO", "NWC"), feature_group_count=CONV_CH)
    y = y + conv_b
    y = _layernorm(y, ln_g, ln_b)
    return jax.nn.silu(y)


def _stick_breaking(q, k, v):
    B, S = q.shape[0], q.shape[1]
    scale = 1.0 / math.sqrt(SB_HEAD_DIM)
    qh = jnp.transpose(q, (0, 2, 1, 3))
    kh = jnp.transpose(k, (0, 2, 1, 3))
    vh = jnp.transpose(v, (0, 2, 1, 3))
    nb = S // Q_BLOCK
    q_blocks = jnp.transpose(qh.reshape(B, SB_HEADS, nb, Q_BLOCK, SB_HEAD_DIM), (2, 0, 1, 3, 4))
    starts = jnp.arange(nb, dtype=jnp.int32) * Q_BLOCK
    key_pos = jnp.arange(S, dtype=jnp.int32)

    def one_block(args):
        qb, t0 = args
        z = jnp.einsum("bhqd,bhkd->bhqk", qb, kh,
                       preferred_element_type=jnp.float32) * scale
        q_pos = t0 + jnp.arange(Q_BLOCK, dtype=jnp.int32)
        mask = key_pos[None, :] < q_pos[:, None]
        log_beta = jax.nn.log_sigmoid(z)
        log_1m_beta = jnp.where(mask, jax.nn.log_sigmoid(-z), 0.0)
        between = lax.cumsum(log_1m_beta, axis=3, reverse=True) - log_1m_beta
        attn = jnp.where(mask, jnp.exp(log_beta + between), 0.0)
        return jnp.einsum("bhqk,bhkd->bhqd", attn.astype(vh.dtype), vh)

    out = lax.map(one_block, (q_blocks, starts))
    out = jnp.transpose(out, (1, 0, 3, 2, 4))
    return out.reshape(B, S, SB_HEADS, SB_HEAD_DIM)


def setup_inputs(seed: int = 0) -> dict:
    key = jax.random.key(seed)
    ks = jax.random.split(key, 16)
    nrm = jax.random.normal
    def gain(k, shape):
        return 1.0 + 0.05 * nrm(k, shape, jnp.float32)
    return {
        "x": nrm(ks[0], (BATCH, SEQ, D_MODEL), jnp.float32),
        "g_pre_mix": gain(ks[1], (DEPTH, D_MODEL)),
        "w_in": nrm(ks[2], (DEPTH, D_MODEL, IN_COLS), jnp.float32) * D_MODEL ** -0.5,
        "conv_w": nrm(ks[3], (DEPTH, CONV_WIDTH, 1, CONV_CH), jnp.float32) * CONV_WIDTH ** -0.5,
        "conv_b": 0.02 * nrm(ks[4], (DEPTH, CONV_CH), jnp.float32),
        "conv_ln_g": gain(ks[5], (DEPTH, CONV_CH)),
        "conv_ln_b": 0.02 * nrm(ks[6], (DEPTH, CONV_CH), jnp.float32),
        "attn_norm_g": gain(ks[7], (DEPTH, SB_HEADS, SB_HEAD_DIM)),
        "w_out": nrm(ks[8], (DEPTH, D_MIX, D_MODEL), jnp.float32) * D_MIX ** -0.5,
        "g_post_mix": gain(ks[9], (DEPTH, D_MODEL)),
        "g_pre_ffn": gain(ks[10], (DEPTH, D_MODEL)),
        "w_gate": nrm(ks[11], (DEPTH, D_MODEL, D_FF), jnp.float32) * D_MODEL ** -0.5,
        "w_up": nrm(ks[12], (DEPTH, D_MODEL, D_FF), jnp.float32) * D_MODEL ** -0.5,
        "w_down": nrm(ks[13], (DEPTH, D_FF, D_MODEL), jnp.float32) * D_FF ** -0.5,
        "g_post_ffn": gain(ks[14], (DEPTH, D_MODEL)),
    }


def reference(x, g_pre_mix, w_in, conv_w, conv_b, conv_ln_g, conv_ln_b, attn_norm_g,
              w_out, g_post_mix, g_pre_ffn, w_gate, w_up, w_down, g_post_ffn):
    B, S, _ = x.shape
    h = x
    for l in range(DEPTH):
        a = _rmsnorm(h, g_pre_mix[l])
        u = jnp.einsum("bsd,dc->bsc", a, w_in[l])
        c_val = u[..., :CONV_CH]
        c_gate = u[..., CONV_CH:2 * CONV_CH]
        qkv = u[..., 2 * CONV_CH:].reshape(B, S, 3, SB_HEADS, SB_HEAD_DIM)
        conv_out = _conformer_conv(c_val, c_gate, conv_w[l], conv_b[l], conv_ln_g[l], conv_ln_b[l])
        attn_out = _stick_breaking(qkv[:, :, 0], qkv[:, :, 1], qkv[:, :, 2])
        attn_out = _rmsnorm(attn_out, attn_norm_g[l]).reshape(B, S, SB_WIDTH)
        mixed = jnp.concatenate([conv_out, attn_out], axis=-1)
        y = jnp.einsum("bsc,cd->bsd", mixed, w_out[l])
        h = h + _rmsnorm(y, g_post_mix[l])
        f_in = _rmsnorm(h, g_pre_ffn[l])
        gt = jnp.einsum("bsd,df->bsf", f_in, w_gate[l])
        up = jnp.einsum("bsd,df->bsf", f_in, w_up[l])
        f = jnp.einsum("bsf,fd->bsd", jax.nn.silu(gt) * up, w_down[l])
        h = h + _rmsnorm(f, g_post_ffn[l])
    return h
```

```python
import numpy as np
from contextlib import ExitStack
import concourse.bass as bass
import concourse.mybir as mybir
from concourse.bass_utils import run_bass_kernel_spmd

F32 = mybir.dt.float32
BF16 = mybir.dt.bfloat16
AF = mybir.ActivationFunctionType
ALU = mybir.AluOpType

D = 1024
S = 8192
NB = 4
CH = 512
NCH = S // CH
NOWN = 8
DFF = 2816
NFT = DFF // 128
EPS = 1e-6
NEG = -30000.0
NW = 4


class Res:
    __slots__ = ("name", "w", "r")

    def __init__(self, name):
        self.name = name
        self.w = None
        self.r = {}


class DSem:
    __slots__ = ("key", "count")

    def __init__(self, key):
        self.key = key
        self.count = 0


class Prog:
    ENGS = ("pe", "act", "dve", "pool", "sp")

    def __init__(self, nc, es):
        self.nc = nc
        self.es = es
        self.sems = {}
        self.items = {e: [] for e in self.ENGS}
        self.n = {e: 0 for e in self.ENGS}
        self.waited = {e: {} for e in self.ENGS}
        for e in self.ENGS:
            self.sems["eng_" + e] = es.enter_context(nc.semaphore("sem_" + e))

    def dsem(self, name):
        key = "d_" + name
        self.sems[key] = self.es.enter_context(self.nc.semaphore("dsem_" + name))
        return DSem(key)

    def _deps(self, eng, reads, writes):
        deps = {}

        def add(tok):
            if tok is None:
                return
            k, c = tok
            if k == "eng_pe" and eng == "pe":
                return
            if deps.get(k, 0) < c:
                deps[k] = c
        for r in reads:
            add(r.w)
        for w in writes:
            add(w.w)
            for k, c in w.r.items():
                add((k, c))
        waits = []
        wd = self.waited[eng]
        for k, c in deps.items():
            if wd.get(k, 0) >= c:
                continue
            wd[k] = c
            waits.append((k, c))
        return waits

    def _commit(self, tok, reads, writes):
        k, c = tok
        for r in reads:
            if r.r.get(k, 0) < c:
                r.r[k] = c
        for w in writes:
            w.w = tok
            w.r = {}

    def op(self, eng, fn, reads=(), writes=()):
        waits = self._deps(eng, reads, writes)
        self.n[eng] += 1
        tok = ("eng_" + eng, self.n[eng])
        self.items[eng].append((waits, fn, "eng_" + eng, 1))
        self._commit(tok, reads, writes)
        return tok

    def dma(self, eng, fn, dsem, reads=(), writes=()):
        waits = self._deps(eng, reads, writes)
        dsem.count += 16
        tok = (dsem.key, dsem.count)
        self.items[eng].append((waits, fn, dsem.key, 16))
        self._commit(tok, reads, writes)
        return tok

    def wait_all(self, eng, toks):
        self.items[eng].append((list(toks), None, None, 0))

    def replay(self, eng, e):
        sems = self.sems
        for waits, fn, skey, inc in self.items[eng]:
            for k, c in waits:
                e.wait_ge(sems[k], c)
            if fn is None:
                continue
            ins = fn(e)
            ins.then_inc(sems[skey], inc)

    def run(self):
        block = self.es.enter_context(self.nc.Block())
        P = self

        @block.tensor
        def _(e):
            P.replay("pe", e)

        @block.scalar
        def _(e):
            P.replay("act", e)

        @block.vector
        def _(e):
            P.replay("dve", e)

        @block.gpsimd
        def _(e):
            P.replay("pool", e)

        @block.sync
        def _(e):
            P.replay("sp", e)


class Buf:
    __slots__ = ("ap", "res")

    def __init__(self, ap, res):
        self.ap = ap
        self.res = res


class Arena:
    def __init__(self, nc, es, name, nbytes, seg=512):
        self.t = es.enter_context(nc.sbuf_tensor(name, [128, nbytes // 4], F32))
        self.seg = seg
        self.nbytes = nbytes
        self.res = [Res("%s_%d" % (name, i)) for i in range((nbytes + seg - 1) // seg)]

    def buf(self, off, nbytes, dtype=F32):
        assert off % 4 == 0 and nbytes % 4 == 0 and off + nbytes <= self.nbytes, (off, nbytes)
        ap = self.t[:, off // 4:(off + nbytes) // 4]
        if dtype == BF16:
            ap = ap.bitcast(BF16)
        s0 = off // self.seg
        s1 = (off + nbytes - 1) // self.seg
        return Buf(ap, self.res[s0:s1 + 1])


def build_program(debug=False, nown=NOWN, nch0=NCH):
    nc = bass.Bass("TRN2", target_bir_lowering=False)
    dbg_out = {}

    def din(name, shape, dt=F32):
        return nc.dram_tensor(name, shape, dt, kind="ExternalInput").ap()

    x_all = din("x_all", [S, D])
    x_own = din("x_own", [NOWN, CH, D])
    x_halo = din("x_halo", [NOWN, 32, D])
    w_in = din("w_in", [D, 2560])
    w_out = din("w_out", [D, D])
    w_gate = din("w_gate", [D, DFF])
    w_up = din("w_up", [D, DFF])
    w_down = din("w_down", [DFF, D])
    gT_d = din("gT", [128, 16])
    gpost_d = din("gpost", [2, D])
    pvec_d = din("pvec", [128, 140])
    bb_d = din("bb", [128, 8])
    cbf_d = din("cbf", [128, 1408])
    cf_d = din("cf", [128, 256])
    out_own = nc.dram_tensor("out_own", [NOWN, CH, D], F32, kind="ExternalOutput").ap()

    w_in_v = w_in.rearrange("(k p) n -> p k n", p=128)
    w_out_v = w_out.rearrange("(k p) n -> p k n", p=128)
    w_gate_v = w_gate.rearrange("(k p) n -> p k n", p=128)
    w_up_v = w_up.rearrange("(k p) n -> p k n", p=128)
    w_down_v = w_down.rearrange("(k p) n -> p k n", p=128)

    with ExitStack() as es:
        E = es.enter_context
        P = Prog(nc, es)
        KT = E(nc.sbuf_tensor("KT", [128, 4, S], BF16))
        V = E(nc.sbuf_tensor("V", [128, 64, 512], BF16))
        KTres = [[Res("KT%d_%d" % (i, h)) for h in range(4)] for i in range(NCH)]
        Vres = [[Res("V%d_%d" % (i, h)) for h in range(4)] for i in range(NCH)]
        R1 = Arena(nc, es, "R1", 24576)
        R2 = Arena(nc, es, "R2", 22528)
        wring = [E(nc.sbuf_tensor("wring%d" % i, [128, 2048], BF16)) for i in range(NW)]
        wres = [Res("wring%d" % i) for i in range(NW)]
        wsem = [P.dsem("w%d" % i) for i in range(NW)]
        gpost = E(nc.sbuf_tensor("gpost_sb", [128, D], F32))
        gpost_res = Res("gpost")
        gpost_sem = P.dsem("gpost")
        sg = E(nc.sbuf_tensor("sg", [128, 2, 512], F32))
        sg_res = [Res("sg0"), Res("sg1")]
        cbf = E(nc.sbuf_tensor("cbf_sb", [128, 1408], BF16))
        cf = E(nc.sbuf_tensor("cf_sb", [128, 256], F32))
        pv = E(nc.sbuf_tensor("pv_sb", [128, 140], F32))
        gTp = E(nc.sbuf_tensor("gTp", [128, 16], F32))
        bbt = E(nc.sbuf_tensor("bbt", [128, 8], F32))
        const_res = Res("consts")
        NST = 12
        st = E(nc.sbuf_tensor("st", [128, 4 * NST], F32))
        st_res = [Res("st%d" % i) for i in range(NST)]
        st_ctr = [0]
        pbig = [E(nc.psum_tensor("pbig%d" % i, [128, 1024], F32)) for i in range(4)]
        pb = [pbig[k // 2][:, (k % 2) * 512:(k % 2 + 1) * 512] for k in range(8)]
        pr = [Res("pb%d" % i) for i in range(8)]

        ident = cbf[:, 0:128]
        triNeg = cbf[:, 128:256]
        restNeg = cbf[:, 256:384]
        flagI = cbf[:, 384:512]
        Mm = cbf[:, 512:1408]
        ones_f = cf[:, 0:128]
        blockones_f = cf[:, 128:256]
        convw = pv[:, 0:124].rearrange("p (c w) -> p c w", c=4)
        conv_b = pv[:, 124:128]
        ln_g = pv[:, 128:132]
        ln_b = pv[:, 132:136]
        g_attn = pv[:, 136:140]
        gT_mix = gTp[:, 0:8]
        gT_ffn = gTp[:, 8:16]

        csem_sw = P.dsem("consts_sw")
        csem_hw = P.dsem("consts_hw")
        const_res_hw = Res("consts_hw")
        P.dma("pool", lambda e: e.dma_start(out=cbf[:], in_=cbf_d), csem_sw, writes=[const_res])
        P.dma("sp", lambda e: e.dma_start(out=cf[:], in_=cf_d), csem_hw, writes=[const_res_hw])
        P.dma("sp", lambda e: e.dma_start(out=pv[:], in_=pvec_d), csem_hw, writes=[const_res_hw])
        P.dma("sp", lambda e: e.dma_start(out=gTp[:], in_=gT_d), csem_hw, writes=[const_res_hw])
        P.dma("sp", lambda e: e.dma_start(out=bbt[:], in_=bb_d), csem_hw, writes=[const_res_hw])
        const_res.w = (csem_sw.key, csem_sw.count)
        const_res_hw.w = (csem_hw.key, csem_hw.count)
        CR = [const_res, const_res_hw]

        def new_st():
            i = st_ctr[0] % NST
            st_ctr[0] += 1
            return st[:, 4 * i:4 * i + 4], st_res[i]

        def rstd_from(stt, sres, scale):
            P.op("act", lambda e: e.activation(out=stt[:, 2:3], in_=stt[:, 0:1], func=AF.Ln, scale=scale, bias=EPS),
                 writes=[sres])
            P.op("act", lambda e: e.activation(out=stt[:, 3:4], in_=stt[:, 2:3], func=AF.Exp, scale=-0.5),
                 writes=[sres])

        units = []
        for j in range(nown):
            for c0 in (0, 512, 256, 768, 1024, 1280):
                units.append(("win", c0))
            for c0 in (0, 256, 512, 768):
                units.append(("wout", c0))
            for k in range(11):
                units.append(("wg", 256 * k))
                units.append(("wu", 256 * k))
            for k in range(11):
                units.append(("wd", k))
        w_issued = [0]
        w_used = [0]

        def w_issue_upto(n):
            while w_issued[0] < min(n, len(units)):
                u = w_issued[0]
                kind, a = units[u]
                slot = u % NW
                if kind == "wd":
                    dst = wring[slot][:].rearrange("p (s n) -> p s n", s=2)
                    src = w_down_v[:, 2 * a:2 * a + 2, :]
                else:
                    dst = wring[slot][:].rearrange("p (k n) -> p k n", k=8)
                    srcv = {"win": w_in_v, "wout": w_out_v, "wg": w_gate_v, "wu": w_up_v}[kind]
                    src = srcv[:, :, a:a + 256]
                P.dma("pool", (lambda d, s: (lambda e: e.dma_start(out=d, in_=s)))(dst, src), wsem[slot], writes=[wres[slot]])
                w_issued[0] += 1

        def w_done():
            w_issue_upto(w_used[0] + NW)

        def w_get(kind, a):
            u = w_used[0]
            assert units[u] == (kind, a), (units[u], kind, a)
            w_issue_upto(u + 1)
            w_used[0] += 1
            slot = u % NW
            if kind == "wd":
                return wring[slot][:].rearrange("p (s n) -> p s n", s=2), wres[slot]
            return wring[slot][:].rearrange("p (k n) -> p k n", k=8), wres[slot]

        xs_sems = [P.dsem("xs0"), P.dsem("xs1")]

        nt_ctr = [0]

        def nt_stages(np_, xsb, xnb, junkb, tbank, gT, dst_ap, dst_res):
            stt, sres = new_st()
            tb = pb[tbank][:, :].bitcast(BF16)

            def stats():
                P.op("act", lambda e: e.activation(out=junkb.ap[0:np_, :], in_=xsb.ap[0:np_, :], func=AF.Square,
                                                   accum_out=stt[0:np_, 0:1]),
                     reads=list(xsb.res), writes=list(junkb.res) + [sres])
                rstd_from(stt[0:np_, :], sres, 1.0 / D)

            nt_ctr[0] += 1
            on_act = (nt_ctr[0] % 2 == 0) and np_ == 128

            def scale():
                if on_act:
                    P.op("act", lambda e: e.activation(out=xnb.ap[0:np_, :], in_=xsb.ap[0:np_, :], func=AF.Identity,
                                                       scale=stt[0:np_, 3:4]),
                         reads=list(xsb.res) + [sres], writes=list(xnb.res))
                else:
                    P.op("dve", lambda e: e.tensor_scalar(out=xnb.ap[0:np_, :], in0=xsb.ap[0:np_, :], scalar1=stt[0:np_, 3:4],
                                                          scalar2=None, op0=ALU.mult),
                         reads=list(xsb.res) + [sres], writes=list(xnb.res))

            def transpose():
                def tr(e):
                    ins = None
                    for k in range(8):
                        ins = e.transpose(tb[:, k * np_:(k + 1) * np_], xnb.ap[0:np_, k * 128:(k + 1) * 128],
                                          ident[0:np_, 0:np_])
                    return ins
                P.op("pe", tr, reads=list(xnb.res) + CR, writes=[pr[tbank]])

            def evac():
                P.op("dve", lambda e: e.tensor_tensor(out=dst_ap, in0=tb[:, 0:8 * np_].rearrange("p (k t) -> p k t", k=8),
                                                      in1=gT.unsqueeze(2).broadcast_to([128, 8, np_]), op=ALU.mult),
                     reads=CR, writes=[pr[tbank]] + list(dst_res))
            return [stats, scale, transpose, evac]

        def norm_transpose(np_, xsb, xnb, junkb, tbank, gT, dst_ap, dst_res):
            for f in nt_stages(np_, xsb, xnb, junkb, tbank, gT, dst_ap, dst_res):
                f()

        def emit_pipelines(pipes):
            T = max(st + nb + len(stg) - 1 for nb, stg, st in pipes)
            for t in range(T):
                for nb, stg, st in pipes:
                    for si in range(len(stg) - 1, -1, -1):
                        blk = t - st - si
                        if 0 <= blk < nb:
                            stg[si](blk)

        def mm_group(e, out, pairs, first=True, last=True):
            n = len(pairs)
            ins = None
            for i, (l, r) in enumerate(pairs):
                ins = e.matmul(out, lhsT=l, rhs=r, start=(first and i == 0), stop=(last and i == n - 1))
            return ins

        wkv = R2.buf(0, 16384, BF16)
        wkv_v = wkv.ap.rearrange("p (k n) -> p k n", k=8)
        wkv_sem = P.dsem("wkv")
        for hf in range(2):
            P.dma("pool", (lambda hf: (lambda e: e.dma_start(out=wkv_v[:, :, hf * 512:(hf + 1) * 512],
                                                             in_=w_in_v[:, :, 1536 + hf * 512:1536 + (hf + 1) * 512])))(hf),
                  wkv_sem, writes=list(wkv.res))
        for r in wkv.res:
            r.w = (wkv_sem.key, wkv_sem.count)
        aT0 = [R1.buf(0, 8192, BF16), R1.buf(8192, 8192, BF16)]
        xs0 = [R1.buf(16384, 4096, F32), R1.buf(20480, 4096, F32)]
        xn0 = [R2.buf(16384, 2048, BF16), R2.buf(18432, 2048, BF16)]
        junk0 = R2.buf(20480, 2048, BF16)
        p0 = {"obank": 2, "evac": 0}
        p0_nts = {}

        def p0_get(g):
            if g not in p0_nts:
                ci, bi = g // 4, g % 4
                aT = aT0[ci % 2]
                aT_v = aT.ap.rearrange("p (k t) -> p k t", k=8)
                sl = g % 2
                p0_nts[g] = nt_stages(128, xs0[sl], xn0[sl], junk0, g % 2, gT_mix, aT_v[:, :, bi * 128:(bi + 1) * 128], aT.res)
            return p0_nts[g]

        def p0_load(g):
            sl = g % 2
            r0 = g * 128
            P.dma("sp", (lambda sl, r0: (lambda e: e.dma_start(out=xs0[sl].ap, in_=x_all[r0:r0 + 128, :])))(sl, r0),
                  xs_sems[sl], writes=list(xs0[sl].res))

        def p0_evac(bk, dst, res):
            if p0["evac"] % 2 == 0:
                P.op("act", (lambda bk, dst: (lambda e: e.activation(out=dst, in_=pb[bk][:, :], func=AF.Identity)))(bk, dst),
                     writes=[pr[bk], res])
            else:
                P.op("dve", (lambda bk, dst: (lambda e: e.tensor_copy(out=dst, in_=pb[bk][:, :])))(bk, dst),
                     writes=[pr[bk], res])
            p0["evac"] += 1

        def p0_mm_group(ci, gi):
            aT = aT0[ci % 2]
            aT_v = aT.ap.rearrange("p (k t) -> p k t", k=8)
            bk = p0["obank"]
            p0["obank"] = 2 + (bk - 2 + 1) % 6
            if gi < 4:
                hp = gi
                P.op("pe", (lambda bk, hp, aT_v: (lambda e: mm_group(
                    e, pb[bk][:, :], [(wkv_v[:, kc, hp * 128:(hp + 1) * 128], aT_v[:, kc, :]) for kc in range(8)])))(bk, hp, aT_v),
                    reads=list(aT.res) + list(wkv.res), writes=[pr[bk]])
                p0_evac(bk, KT[:, hp, ci * CH:(ci + 1) * CH], KTres[ci][hp])
            else:
                bi = gi - 4
                P.op("pe", (lambda bk, bi, aT_v: (lambda e: mm_group(
                    e, pb[bk][:, :], [(aT_v[:, kc, bi * 128:(bi + 1) * 128], wkv_v[:, kc, 512:1024]) for kc in range(8)])))(bk, bi, aT_v),
                    reads=list(aT.res) + list(wkv.res), writes=[pr[bk]])
                p0_evac(bk, V[:, ci * 4 + bi, :], Vres[ci][bi])

        NG = 4 * nch0
        for t in range(NG + 12):
            for si in (4, 3, 2, 1):
                g = t - si
                if 0 <= g < NG:
                    p0_get(g)[si - 1]()
            if t < NG:
                p0_load(t)
            c = (t - 8) // 4
            if t >= 8 and c < nch0:
                for gi in ((t - 8) % 4 * 2, (t - 8) % 4 * 2 + 1):
                    p0_mm_group(c, gi)

        aT1 = R1.buf(16384, 8192, BF16)
        aT1_v = aT1.ap.rearrange("p (k t) -> p k t", k=8)
        xs1 = [R2.buf(0, 4096, F32), R2.buf(4096, 4096, F32)]
        xn1 = [R2.buf(8192, 2048, BF16), R2.buf(10240, 2048, BF16)]
        junk1 = R2.buf(12288, 2048, BF16)
        aTh = R2.buf(20992, 512, BF16)
        aTh_v = aTh.ap.rearrange("p (k t) -> p k t", k=8)
        yT = [R1.buf(ct * 2048, 2048, F32) for ct in range(4)]
        ysq = [R1.buf(8192 + ct * 2048, 2048, F32) for ct in range(4)]
        diagA = R1.buf(16384, 4096, BF16)
        diagA_v = diagA.ap.rearrange("p (w c) -> p w c", w=16)
        diagB = R1.buf(20480, 3840, BF16)
        diagB_v = diagB.ap.rearrange("p (w c) -> p w c", w=15)
        e_ring = [R1.buf(i * 4096, 4096, F32) for i in range(3)]
        sp_ring = [R1.buf(12288 + i * 2048, 2048, BF16) for i in range(3)]
        ec_ring = [R1.buf(18432 + i * 2048, 2048, BF16) for i in range(2)]
        osb = R1.buf(22528, 2048, F32)
        hb = [R1.buf(i * 4096, 4096, F32) for i in range(4)]
        fT = R1.buf(16384, 8192, BF16)
        fT_v = fT.ap.rearrange("p (k t) -> p k t", k=8)
        stage = [R1.buf(16384, 4096, F32), R1.buf(20480, 4096, F32)]
        mixedT = [R2.buf(ct * 1024, 1024, BF16) for ct in range(8)]
        QT = [R2.buf(8192 + hp * 1024, 1024, BF16) for hp in range(4)]
        gluT = [R2.buf(12288 + ct * 1088, 1088, BF16) for ct in range(4)]
        sigS = [R2.buf(16640, 2176, F32), R2.buf(18816, 2176, F32)]
        junk2b = R2.buf(16384, 2048, BF16)
        at_ring = [R2.buf(12288 + i * 2048, 2048, BF16) for i in range(2)]
        sqb = R2.buf(16384, 2048, F32)
        rsb = R2.buf(18432, 2048, F32)
        hn = [R2.buf(0, 2048, BF16), R2.buf(2048, 2048, BF16)]
        xs2 = [R2.buf(8192, 4096, F32), R2.buf(12288, 4096, F32)]
        junk3 = R2.buf(4096, 2048, BF16)
        actT = [R2.buf(ft * 1024, 1024, BF16) for ft in range(NFT)]
        junk4b = R2.buf(14336, 2048, BF16)
        out_sems = [P.dsem("out%d" % i) for i in range(4)]
        dbg_sem = P.dsem("dbg") if debug else None

        def dump(name, ap, res, dt=BF16):
            if not debug:
                return
            shp = list(ap.shape)
            d = nc.dram_tensor("dbg_" + name, shp, dt, kind="ExternalOutput").ap()
            dbg_out[name] = d
            P.dma("sp", (lambda d, ap: (lambda e: e.dma_start(out=d, in_=ap)))(d, ap), dbg_sem, reads=list(res))
        out_toks = {}
        xblk = 0

        def phase_B_pipe(j):
            nts = {}

            def get(i):
                if i not in nts:
                    sl = i % 2
                    if i == 0:
                        nts[i] = nt_stages(32, xs1[sl], xn1[sl], junk1, 0, gT_mix, aTh_v[:, :, :], aTh.res)
                    else:
                        bi = i - 1
                        nts[i] = nt_stages(128, xs1[sl], xn1[sl], junk1, bi % 2, gT_mix, aT1_v[:, :, bi * 128:(bi + 1) * 128], aT1.res)
                return nts[i]

            def load(i):
                sl = i % 2
                if i == 0:
                    P.dma("sp", (lambda sl: (lambda e: e.dma_start(out=xs1[sl].ap[0:32, :], in_=x_halo[j])))(sl),
                          xs_sems[sl], writes=list(xs1[sl].res))
                else:
                    bi = i - 1
                    P.dma("sp", (lambda sl, bi: (lambda e: e.dma_start(out=xs1[sl].ap, in_=x_own[j, bi * 128:(bi + 1) * 128, :])))(sl, bi),
                          xs_sems[sl], writes=list(xs1[sl].res))
            return (5, [load, lambda i: get(i)[0](), lambda i: get(i)[1](), lambda i: get(i)[2](), lambda i: get(i)[3]()], 0)

        w_done()
        emit_pipelines([phase_B_pipe(0)])

        for j in range(nown):
            if j == 0:
                dump("aT", aT1.ap, aT1.res)
                dump("aTh", aTh.ap, aTh.res)
                dump("KT", KT[:, :, 0:1024], [r for ci in range(2) for r in KTres[ci]])
                dump("V", V[:, 0:8, :], [r for ci in range(2) for r in Vres[ci]])
            for half in range(2):
                if half == 1:
                    w_done()
                wv, wvr = w_get("win", 0 + 256 * half)
                wg_, wgr = w_get("win", 512 + 256 * half)
                for sub in range(2):
                    ct = half * 2 + sub
                    bs = (2, 3, 4) if ct % 2 == 0 else (5, 6, 7)
                    for (wt, wr, bmain, hoff) in ((wv, wvr, bs[0], 0), (wg_, wgr, bs[1], 32)):
                        P.op("pe", (lambda wt, sub, bmain, hb_, hoff: (lambda e: (
                            mm_group(e, pb[bmain][:, :], [(wt[:, kc, sub * 128:(sub + 1) * 128], aT1_v[:, kc, :]) for kc in range(8)]),
                            mm_group(e, pb[hb_][:, hoff:hoff + 32], [(wt[:, kc, sub * 128:(sub + 1) * 128], aTh_v[:, kc, :]) for kc in range(8)]))[1]))(
                                wt, sub, bmain, bs[2], hoff),
                            reads=[wr] + list(aT1.res) + list(aTh.res), writes=[pr[bmain], pr[bs[2]]])
                    sS = sigS[ct % 2]
                    P.op("act", (lambda sS, b: (lambda e: e.activation(out=sS.ap[:, 32:544], in_=pb[b][:, :], func=AF.Sigmoid)))(sS, bs[1]),
                         writes=[pr[bs[1]]] + list(sS.res))
                    P.op("act", (lambda sS, b: (lambda e: e.activation(out=sS.ap[:, 0:32], in_=pb[b][:, 32:64], func=AF.Sigmoid)))(sS, bs[2]),
                         writes=[pr[bs[2]]] + list(sS.res))
                    P.op("dve", (lambda sS, b, ct: (lambda e: e.tensor_tensor(out=gluT[ct].ap[:, 32:544], in0=pb[b][:, :], in1=sS.ap[:, 32:544], op=ALU.mult)))(sS, bs[0], ct),
                         reads=list(sS.res), writes=[pr[bs[0]]] + list(gluT[ct].res))
                    P.op("dve", (lambda sS, b, ct: (lambda e: e.tensor_tensor(out=gluT[ct].ap[:, 0:32], in0=pb[b][:, 0:32], in1=sS.ap[:, 0:32], op=ALU.mult)))(sS, bs[2], ct),
                         reads=list(sS.res), writes=[pr[bs[2]]] + list(gluT[ct].res))
            w_done()
            for half in range(2):
                if half == 1:
                    w_done()
                wq, wqr = w_get("win", 1024 + 256 * half)
                for sub in range(2):
                    hp = half * 2 + sub
                    bk = 2 + hp
                    P.op("pe", (lambda wq, sub, bk: (lambda e: mm_group(
                        e, pb[bk][:, :], [(wq[:, kc, sub * 128:(sub + 1) * 128], aT1_v[:, kc, :]) for kc in range(8)])))(wq, sub, bk),
                        reads=[wqr] + list(aT1.res), writes=[pr[bk]])
                    P.op("dve", (lambda bk, hp: (lambda e: e.tensor_scalar(out=QT[hp].ap, in0=pb[bk][:, :], scalar1=0.125, scalar2=None, op0=ALU.mult)))(bk, hp),
                         writes=[pr[bk]] + list(QT[hp].res))

            w_done()
            if j == 0:
                for ct in range(4):
                    dump("glu%d" % ct, gluT[ct].ap, gluT[ct].res)
                    dump("QT%d" % ct, QT[ct].ap, QT[ct].res)
            for ct in range(4):
                bk = 2 + ct
                for (dbuf, dv, w0, nw) in ((diagA, diagA_v, 0, 16), (diagB, diagB_v, 16, 15)):
                    P.op("dve", (lambda dv, w0, nw, ct: (lambda e: e.tensor_tensor(
                        out=dv, in0=ident.unsqueeze(1).broadcast_to([128, nw, 128]),
                        in1=convw[:, ct, w0:w0 + nw].unsqueeze(2).broadcast_to([128, nw, 128]), op=ALU.mult)))(dv, w0, nw, ct),
                        reads=CR, writes=list(dbuf.res))
                    P.op("pe", (lambda bk, ct, dv, w0, nw: (lambda e: mm_group(
                        e, pb[bk][:, :], [(dv[:, w - w0, :], gluT[ct].ap[:, w + 2:w + 2 + 512]) for w in range(w0, w0 + nw)],
                        first=(w0 == 0), last=(w0 + nw == 31))))(bk, ct, dv, w0, nw),
                        reads=list(dbuf.res) + list(gluT[ct].res), writes=[pr[bk]])
                P.op("act", (lambda bk, ct: (lambda e: e.activation(out=yT[ct].ap, in_=pb[bk][:, :], func=AF.Identity, bias=conv_b[:, ct:ct + 1])))(bk, ct),
                     reads=CR, writes=[pr[bk]] + list(yT[ct].res))
                P.op("act", (lambda ct: (lambda e: e.activation(out=ysq[ct].ap, in_=yT[ct].ap, func=AF.Square)))(ct),
                     reads=list(yT[ct].res), writes=list(ysq[ct].res))
            P.op("pe", lambda e: mm_group(e, pb[6][:, :], [(ones_f, yT[ct].ap) for ct in range(4)]),
                 reads=CR + [r for ct in range(4) for r in yT[ct].res], writes=[pr[6]])
            P.op("pe", lambda e: mm_group(e, pb[7][:, :], [(ones_f, ysq[ct].ap) for ct in range(4)]),
                 reads=CR + [r for ct in range(4) for r in ysq[ct].res], writes=[pr[7]])
            meanb, msqb, varb = ysq[0], ysq[1], ysq[2]
            P.op("dve", lambda e: e.tensor_scalar(out=meanb.ap, in0=pb[6][:, :], scalar1=1.0 / 512, scalar2=None, op0=ALU.mult),
                 writes=[pr[6]] + list(meanb.res))
            P.op("dve", lambda e: e.tensor_tensor(out=msqb.ap, in0=meanb.ap, in1=meanb.ap, op=ALU.mult),
                 reads=list(meanb.res), writes=list(msqb.res))
            P.op("dve", lambda e: e.scalar_tensor_tensor(out=varb.ap, in0=pb[7][:, :], scalar=1.0 / 512, in1=msqb.ap, op0=ALU.mult, op1=ALU.subtract),
                 reads=list(msqb.res), writes=[pr[7]] + list(varb.res))
            P.op("act", lambda e: e.activation(out=varb.ap, in_=varb.ap, func=AF.Ln, bias=EPS), writes=list(varb.res))
            P.op("act", lambda e: e.activation(out=varb.ap, in_=varb.ap, func=AF.Exp, scale=-0.5), writes=list(varb.res))
            for ct in range(4):
                P.op("dve", (lambda ct: (lambda e: e.tensor_tensor(out=yT[ct].ap, in0=yT[ct].ap, in1=meanb.ap, op=ALU.subtract)))(ct),
                     reads=list(meanb.res), writes=list(yT[ct].res))
                P.op("dve", (lambda ct: (lambda e: e.tensor_tensor(out=yT[ct].ap, in0=yT[ct].ap, in1=varb.ap, op=ALU.mult)))(ct),
                     reads=list(varb.res), writes=list(yT[ct].res))
                P.op("act", (lambda ct: (lambda e: e.activation(out=mixedT[ct].ap, in_=yT[ct].ap, func=AF.Silu,
                                                                  scale=ln_g[:, ct:ct + 1], bias=ln_b[:, ct:ct + 1])))(ct),
                     reads=CR + list(yT[ct].res), writes=list(mixedT[ct].res))

            nkb = 8 * (j + 1)
            items = []
            for hp in range(4):
                for kb in range(nkb - 1, -1, -1):
                    items.append((hp, kb, kb == nkb - 1, kb == 0))
            N = len(items)
            deferred = {}
            ZSL = [(0, 1), (2, 3)]
            CB = (4, 5)
            OB = 6
            SB = 7

            def q0_of(i):
                kbl = items[i][1] - 8 * j
                return 128 * (kbl - 4) if kbl >= 4 else 0

            def v3(ap2d, q0):
                return ap2d.rearrange("p (a n) -> p a n", a=2)[:, :, q0:512]

            def Zop(i):
                hp, kb, first, last = items[i]
                za, zb = ZSL[i % 2]
                kbl = kb - 8 * j
                q0 = q0_of(i)

                def fn(e, hp=hp, kb=kb, kbl=kbl, za=za, zb=zb, q0=q0):
                    ins = None
                    for hl, bk in ((0, za), (1, zb)):
                        p0_ = 64 * hl
                        ins = e.matmul(pb[bk][:, q0:512], lhsT=KT[p0_:p0_ + 64, hp, kb * 128:(kb + 1) * 128], rhs=QT[hp].ap[p0_:p0_ + 64, q0:512],
                                       start=True, stop=(kbl < 0))
                    if kbl >= 0:
                        o = 384 - 128 * (kbl % 4)
                        for bk in (za, zb):
                            ins = e.matmul(pb[bk][:, q0:512], lhsT=(flagI if kbl < 4 else ident), rhs=Mm[:, o + q0:o + 512], start=False, stop=True)
                    return ins
                P.op("pe", fn, reads=[KTres[kb // 4][hp]] + list(QT[hp].res) + CR, writes=[pr[za], pr[zb]])

            def Ezop(i):
                hp, kb, first, last = items[i]
                za, zb = ZSL[i % 2]
                kbl = kb - 8 * j
                q0 = q0_of(i)
                eb = e_ring[i % 3]
                bias = bbt[:, kbl:kbl + 1] if kbl >= 0 else 0.0
                if q0 == 0:
                    P.op("act", (lambda za, eb, bias: (lambda e: e.activation(out=eb.ap, in_=pbig[za // 2][:, :], func=AF.Exp, bias=bias)))(za, eb, bias),
                         reads=CR, writes=[pr[za], pr[zb]] + list(eb.res))
                else:
                    def fn(e, za=za, zb=zb, eb=eb, bias=bias, q0=q0):
                        e.activation(out=eb.ap[:, q0:512], in_=pb[za][:, q0:512], func=AF.Exp, bias=bias)
                        return e.activation(out=eb.ap[:, 512 + q0:1024], in_=pb[zb][:, q0:512], func=AF.Exp, bias=bias)
                    P.op("act", fn, reads=CR, writes=[pr[za], pr[zb]] + list(eb.res))

            def Lnop(i):
                eb = e_ring[i % 3]
                sb = sp_ring[i % 3]
                q0 = q0_of(i)
                if q0 == 0:
                    P.op("act", (lambda eb, sb: (lambda e: e.activation(out=sb.ap, in_=eb.ap, func=AF.Ln, bias=1.0)))(eb, sb),
                         reads=list(eb.res), writes=list(sb.res))
                else:
                    def fn(e, eb=eb, sb=sb, q0=q0):
                        e.activation(out=sb.ap[:, q0:512], in_=eb.ap[:, q0:512], func=AF.Ln, bias=1.0)
                        return e.activation(out=sb.ap[:, 512 + q0:1024], in_=eb.ap[:, 512 + q0:1024], func=AF.Ln, bias=1.0)
                    P.op("act", fn, reads=list(eb.res), writes=list(sb.res))

            def Triop(i):
                first = items[i][2]
                sb = sp_ring[i % 3]
                q0 = q0_of(i)

                def fn(e, sb=sb, first=first, q0=q0):
                    e.matmul(pb[CB[0]][:, q0:512], lhsT=triNeg, rhs=sb.ap[:, q0:512], start=first, stop=True, skip_group_check=True)
                    return e.matmul(pb[CB[1]][:, q0:512], lhsT=triNeg, rhs=sb.ap[:, 512 + q0:1024], start=first, stop=True, skip_group_check=True)
                P.op("pe", fn, reads=list(sb.res) + CR, writes=[pr[CB[0]], pr[CB[1]]])

            def ECop(i):
                xb = ec_ring[i % 2]
                q0 = q0_of(i)
                if q0 == 0:
                    P.op("act", (lambda xb: (lambda e: e.activation(out=xb.ap, in_=pbig[CB[0] // 2][:, :], func=AF.Exp)))(xb),
                         writes=[pr[CB[0]], pr[CB[1]]] + list(xb.res))
                else:
                    def fn(e, xb=xb, q0=q0):
                        e.activation(out=xb.ap[:, q0:512], in_=pb[CB[0]][:, q0:512], func=AF.Exp)
                        return e.activation(out=xb.ap[:, 512 + q0:1024], in_=pb[CB[1]][:, q0:512], func=AF.Exp)
                    P.op("act", fn, writes=[pr[CB[0]], pr[CB[1]]] + list(xb.res))

            def Restop(i):
                sb = sp_ring[i % 3]
                q0 = q0_of(i)

                def fn(e, sb=sb, q0=q0):
                    e.matmul(pb[CB[0]][:, q0:512], lhsT=restNeg, rhs=sb.ap[:, q0:512], start=False, stop=True, skip_group_check=True)
                    return e.matmul(pb[CB[1]][:, q0:512], lhsT=restNeg, rhs=sb.ap[:, 512 + q0:1024], start=False, stop=True, skip_group_check=True)
                P.op("pe", fn, reads=list(sb.res) + CR, writes=[pr[CB[0]], pr[CB[1]]])

            def Mulop(i):
                eb = e_ring[i % 3]
                xb = ec_ring[i % 2]
                ab = at_ring[i % 2]
                q0 = q0_of(i)
                if q0 == 0:
                    P.op("dve", (lambda eb, xb, ab: (lambda e: e.tensor_tensor(out=ab.ap, in0=eb.ap, in1=xb.ap, op=ALU.mult)))(eb, xb, ab),
                         reads=list(eb.res) + list(xb.res), writes=list(ab.res))
                else:
                    def fn(e, eb=eb, xb=xb, ab=ab, q0=q0):
                        e.tensor_tensor(out=ab.ap[:, q0:512], in0=eb.ap[:, q0:512], in1=xb.ap[:, q0:512], op=ALU.mult)
                        return e.tensor_tensor(out=ab.ap[:, 512 + q0:1024], in0=eb.ap[:, 512 + q0:1024], in1=xb.ap[:, 512 + q0:1024], op=ALU.mult)
                    P.op("dve", fn, reads=list(eb.res) + list(xb.res), writes=list(ab.res))

            def AVop(i):
                hp, kb, first, last = items[i]
                ab = at_ring[i % 2]
                c0 = hp * 128
                q0 = q0_of(i)

                def fn(e, kb=kb, c0=c0, ab=ab, first=first, last=last, q0=q0):
                    e.matmul(pb[OB][0:64, q0:512], lhsT=V[:, kb, c0:c0 + 64], rhs=ab.ap[:, q0:512], start=first, stop=last, skip_group_check=True)
                    return e.matmul(pb[OB][64:128, q0:512], lhsT=V[:, kb, c0 + 64:c0 + 128], rhs=ab.ap[:, 512 + q0:1024], start=first, stop=last,
                                    skip_group_check=True)
                P.op("pe", fn, reads=[Vres[kb // 4][kb % 4]] + list(ab.res), writes=[pr[OB]])
                if last:
                    P.op("dve", lambda e: e.tensor_copy(out=osb.ap, in_=pb[OB][:, :]), writes=[pr[OB]] + list(osb.res))
                    deferred.setdefault(min(i + 2, N), []).append(lambda hp=hp: headnorm1(hp))
                    deferred.setdefault(min(i + 3, N), []).append(lambda hp=hp: headnorm2(hp))

            def headnorm1(hp):
                P.op("dve", lambda e: e.tensor_tensor(out=sqb.ap, in0=osb.ap, in1=osb.ap, op=ALU.mult),
                     reads=list(osb.res), writes=list(sqb.res))
                P.op("pe", lambda e: e.matmul(pb[SB][:, :], lhsT=blockones_f, rhs=sqb.ap, start=True, stop=True),
                     reads=list(sqb.res) + CR, writes=[pr[SB]])

            def headnorm2(hp):
                P.op("act", lambda e: e.activation(out=rsb.ap, in_=pb[SB][:, :], func=AF.Ln, scale=1.0 / 64, bias=EPS),
                     writes=[pr[SB]] + list(rsb.res))
                P.op("act", lambda e: e.activation(out=rsb.ap, in_=rsb.ap, func=AF.Exp, scale=-0.5), writes=list(rsb.res))
                P.op("dve", (lambda hp: (lambda e: e.scalar_tensor_tensor(out=mixedT[4 + hp].ap, in0=osb.ap, scalar=g_attn[:, hp:hp + 1],
                                                                            in1=rsb.ap, op0=ALU.mult, op1=ALU.mult)))(hp),
                     reads=list(rsb.res) + list(osb.res) + CR, writes=list(mixedT[4 + hp].res))

            Zop(0)
            Zop(1)
            Ezop(0)
            Lnop(0)
            Ezop(1)
            Lnop(1)
            Triop(0)
            for s_ in range(N):
                if s_ + 2 < N:
                    Zop(s_ + 2)
                ECop(s_)
                if not items[s_][3]:
                    Restop(s_)
                if s_ + 1 < N:
                    Triop(s_ + 1)
                if s_ + 2 < N:
                    Ezop(s_ + 2)
                    Lnop(s_ + 2)
                Mulop(s_)
                if s_ >= 1:
                    AVop(s_ - 1)
                for f in deferred.pop(s_, []):
                    f()
            AVop(N - 1)
            for k in sorted(deferred):
                for f in deferred[k]:
                    f()

            if j == 0:
                for ct in range(8):
                    dump("mixed%d" % ct, mixedT[ct].ap, mixedT[ct].res)
            P.dma("sp", lambda e: e.dma_start(out=gpost[:], in_=gpost_d[0:1, :].broadcast_to([128, D])), gpost_sem, writes=[gpost_res])
            wo_units = [w_get("wout", 256 * u) for u in range(4)]

            def OP(blk):
                for u in range(4):
                    wo, wor = wo_units[u]
                    bk = blk * 2 + u // 2
                    c0 = (u % 2) * 256
                    P.op("pe", (lambda wo, blk, bk, c0: (lambda e: mm_group(
                        e, pb[bk][:, c0:c0 + 256], [(mixedT[ct].ap[:, blk * 128:(blk + 1) * 128], wo[:, ct, :]) for ct in range(8)])))(wo, blk, bk, c0),
                        reads=[wor] + [r for ct in range(8) for r in mixedT[ct].res], writes=[pr[bk]])
                if blk == 3:
                    w_done()
            fstate = {}

            def F0(blk, j=j):
                sl = blk % 2
                P.dma("sp", (lambda sl, blk: (lambda e: e.dma_start(out=xs2[sl].ap, in_=x_own[j, blk * 128:(blk + 1) * 128, :])))(sl, blk),
                      xs_sems[sl], writes=list(xs2[sl].res))
                stt, sres = new_st()
                fstate[blk] = (stt, sres)
                P.op("act", (lambda blk, stt: (lambda e: e.activation(out=junk2b.ap, in_=pbig[blk][:, :], func=AF.Square, accum_out=stt[:, 0:1])))(blk, stt),
                     writes=[pr[2 * blk], pr[2 * blk + 1], sres] + list(junk2b.res))

            def F1(blk):
                stt, sres = fstate[blk]
                rstd_from(stt, sres, 1.0 / D)

            def F2(blk):
                stt, sres = fstate[blk]
                sl = blk % 2
                P.op("dve", (lambda blk, stt: (lambda e: e.scalar_tensor_tensor(
                    out=hb[blk].ap, in0=pbig[blk][:, :], scalar=stt[:, 3:4], in1=gpost[:, :], op0=ALU.mult, op1=ALU.mult)))(blk, stt),
                    reads=[sres, gpost_res], writes=[pr[2 * blk], pr[2 * blk + 1]] + list(hb[blk].res))
                P.op("dve", (lambda blk, sl: (lambda e: e.tensor_tensor(out=hb[blk].ap, in0=hb[blk].ap, in1=xs2[sl].ap, op=ALU.add)))(blk, sl),
                     reads=list(xs2[sl].res), writes=list(hb[blk].res))

            gnt = {}

            def G(blk):
                if blk not in gnt:
                    gnt[blk] = nt_stages(128, hb[blk], hn[blk % 2], junk3, blk % 2, gT_ffn, fT_v[:, :, blk * 128:(blk + 1) * 128], fT.res)
                return gnt[blk]
            emit_pipelines([(4, [OP, F0, F1, F2, lambda b: G(b)[0](), lambda b: G(b)[1](), lambda b: G(b)[2](), lambda b: G(b)[3]()], 0)])

            if j == 0:
                for blk in range(4):
                    dump("h%d" % blk, hb[blk].ap, hb[blk].res, F32)

            if j == 0:
                dump("fT", fT.ap, fT.res)
            pair = 0
            for k in range(11):
                wgt, wgr = w_get("wg", 256 * k)
                wut, wur = w_get("wu", 256 * k)
                for sub in range(2):
                    ft = 2 * k + sub
                    gb = (pair % 4) * 2
                    ub = gb + 1
                    ss_ = pair % 2
                    pair += 1
                    P.op("pe", (lambda wgt, sub, gb: (lambda e: mm_group(
                        e, pb[gb][:, :], [(wgt[:, kc, sub * 128:(sub + 1) * 128], fT_v[:, kc, :]) for kc in range(8)])))(wgt, sub, gb),
                        reads=[wgr] + list(fT.res), writes=[pr[gb]])
                    P.op("pe", (lambda wut, sub, ub: (lambda e: mm_group(
                        e, pb[ub][:, :], [(wut[:, kc, sub * 128:(sub + 1) * 128], fT_v[:, kc, :]) for kc in range(8)])))(wut, sub, ub),
                        reads=[wur] + list(fT.res), writes=[pr[ub]])
                    P.op("act", (lambda gb, ss_: (lambda e: e.activation(out=sg[:, ss_, :], in_=pb[gb][:, :], func=AF.Silu)))(gb, ss_),
                         writes=[pr[gb], sg_res[ss_]])
                    P.op("dve", (lambda ub, ss_, ft: (lambda e: e.tensor_tensor(out=actT[ft].ap, in0=pb[ub][:, :], in1=sg[:, ss_, :], op=ALU.mult)))(ub, ss_, ft),
                         reads=[sg_res[ss_]], writes=[pr[ub]] + list(actT[ft].res))
                w_done()

            if j == 0:
                for ft in (0, 1, 21):
                    dump("act%d" % ft, actT[ft].ap, actT[ft].res)
            for k in range(11):
                wdt, wdr = w_get("wd", k)
                for sub in range(2):
                    ft = 2 * k + sub

                    def fn(e, wdt=wdt, sub=sub, ft=ft):
                        ins = None
                        for blk in range(4):
                            for hf in range(2):
                                ins = e.matmul(pb[blk * 2 + hf][:, :], lhsT=actT[ft].ap[:, blk * 128:(blk + 1) * 128],
                                               rhs=wdt[:, sub, hf * 512:(hf + 1) * 512], start=(ft == 0), stop=(ft == NFT - 1))
                        return ins
                    P.op("pe", fn, reads=[wdr] + list(actT[ft].res), writes=list(pr))
                w_done()

            P.dma("sp", lambda e: e.dma_start(out=gpost[:], in_=gpost_d[1:2, :].broadcast_to([128, D])), gpost_sem, writes=[gpost_res])
            sg_flat = sg[:].rearrange("p a n -> p (a n)")
            jstate = {}

            def J0(blk):
                stt, sres = new_st()
                jstate[blk] = (stt, sres)
                P.op("act", (lambda blk, stt: (lambda e: e.activation(out=junk4b.ap, in_=pbig[blk][:, :], func=AF.Square, accum_out=stt[:, 0:1])))(blk, stt),
                     writes=[pr[2 * blk], pr[2 * blk + 1], sres] + list(junk4b.res))

            def J1(blk):
                stt, sres = jstate[blk]
                rstd_from(stt, sres, 1.0 / D)

            def J2(blk, j=j):
                stt, sres = jstate[blk]
                P.op("dve", (lambda blk, stt: (lambda e: e.scalar_tensor_tensor(
                    out=sg_flat, in0=pbig[blk][:, :], scalar=stt[:, 3:4], in1=gpost[:, :], op0=ALU.mult, op1=ALU.mult)))(blk, stt),
                    reads=[sres, gpost_res], writes=[pr[2 * blk], pr[2 * blk + 1]] + list(sg_res))
                P.op("dve", (lambda blk: (lambda e: e.tensor_tensor(out=hb[blk].ap, in0=hb[blk].ap, in1=sg_flat, op=ALU.add)))(blk),
                     reads=list(sg_res), writes=list(hb[blk].res))
                tok = P.dma("sp", (lambda blk, j: (lambda e: e.dma_start(out=out_own[j, blk * 128:(blk + 1) * 128, :], in_=hb[blk].ap)))(blk, j),
                            out_sems[blk], reads=list(hb[blk].res))
                out_toks[tok[0]] = max(out_toks.get(tok[0], 0), tok[1])
            pipes = [(4, [J0, J1, J2], 0)]
            if j + 1 < nown:
                pipes.append(phase_B_pipe(j + 1))
            emit_pipelines(pipes)

        if debug:
            out_toks[dbg_sem.key] = dbg_sem.count
        P.wait_all("sp", list(out_toks.items()))
        P.run()
    nc._dbg_names = list(dbg_out)
    return nc


_NC_CACHE = {}


def _get_program():
    if "nc" not in _NC_CACHE:
        _NC_CACHE["nc"] = build_program()
    return _NC_CACHE["nc"]


def _consts(half):
    idn = np.eye(128, dtype=np.float32)
    jj = np.arange(128)[:, None]
    ss = np.arange(128)[None, :]
    tri_neg = np.where(jj >= ss, -1.0, 0.0).astype(np.float32)
    rest_neg = np.where(jj < ss, -1.0, 0.0).astype(np.float32)
    flag = idn * (1.0 if half == 0 else 0.0)
    u = np.arange(896)[None, :]
    s_ = np.arange(128)[:, None]
    mm = np.where(s_ >= u - 384, NEG, 0.0).astype(np.float32)
    cbf = np.concatenate([idn, tri_neg, rest_neg, flag, mm], axis=1).astype(np.float32)
    ones = np.ones((128, 128), np.float32)
    blockones = np.zeros((128, 128), np.float32)
    blockones[:64, :64] = 1.0
    blockones[64:, 64:] = 1.0
    cf = np.concatenate([ones, blockones], axis=1)
    bb = np.zeros((128, 8), np.float32)
    if half == 0:
        bb[:, 4:8] = NEG
    return cbf, cf, bb


def kernel(x, g_pre_mix, w_in, conv_w, conv_b, conv_ln_g, conv_ln_b, attn_norm_g, w_out, g_post_mix,
           g_pre_ffn, w_gate, w_up, w_down, g_post_ffn):
    f32 = np.float32
    x = np.asarray(x, f32)
    nc = _get_program()

    def pk(v, n):
        return np.ascontiguousarray(np.asarray(v, f32).reshape(n, 128).T)

    gT = np.concatenate([pk(g_pre_mix[0], 8), pk(g_pre_ffn[0], 8)], axis=1)
    gpost = np.ascontiguousarray(np.stack([np.asarray(g_post_mix[0], f32), np.asarray(g_post_ffn[0], f32)], 0))
    cw = np.asarray(conv_w, f32)[0, :, 0, :]
    convwT = cw.T.reshape(4, 128, 31).transpose(1, 0, 2).reshape(128, 124)
    pvec = np.ascontiguousarray(np.concatenate([convwT, pk(conv_b[0], 4), pk(conv_ln_g[0], 4), pk(conv_ln_b[0], 4),
                                                pk(np.asarray(attn_norm_g, f32)[0].reshape(-1), 4)], axis=1).astype(f32))
    shared = {
        "w_in": np.ascontiguousarray(np.asarray(w_in, f32)[0]),
        "w_out": np.ascontiguousarray(np.asarray(w_out, f32)[0]),
        "w_gate": np.ascontiguousarray(np.asarray(w_gate, f32)[0]),
        "w_up": np.ascontiguousarray(np.asarray(w_up, f32)[0]),
        "w_down": np.ascontiguousarray(np.asarray(w_down, f32)[0]),
        "gT": gT, "gpost": gpost, "pvec": pvec,
    }
    in_maps = []
    for c in range(8):
        b, half = c // 2, c % 2
        xb = x[b]
        xc = xb.reshape(NCH, CH, D)
        own = [2 * j + half for j in range(NOWN)]
        x_own = np.ascontiguousarray(xc[own])
        x_halo = np.zeros((NOWN, 32, D), f32)
        for j, ci in enumerate(own):
            if ci > 0:
                x_halo[j] = xb[ci * CH - 32:ci * CH]
        cbf, cf, bb = _consts(half)
        m = dict(shared)
        m.update({"x_all": np.ascontiguousarray(xb), "x_own": x_own, "x_halo": x_halo, "cbf": cbf, "cf": cf, "bb": bb})
        in_maps.append(m)
    if _NC_CACHE.get("debug_hook") is not None:
        return _NC_CACHE["debug_hook"](in_maps)
    res = run_bass_kernel_spmd(nc, in_maps, core_ids=list(range(8)))
    out = np.empty((NB, S, D), f32)
    for c in range(8):
        b, half = c // 2, c % 2
        oc = np.asarray(res.results[c]["out_own"], f32)
        ov = out[b].reshape(NCH, CH, D)
        for j in range(NOWN):
            ov[2 * j + half] = oc[j]
    return out
```

```python
import numpy as np
from contextlib import ExitStack
import concourse.bass as bass
import concourse.mybir as mybir
from concourse.bass_utils import run_bass_kernel_spmd

F32 = mybir.dt.float32
BF16 = mybir.dt.bfloat16
AF = mybir.ActivationFunctionType
ALU = mybir.AluOpType

D = 1024
S = 8192
NB = 4
CH = 512
NCH = S // CH
NOWN = 8
DFF = 2816
NFT = DFF // 128
EPS = 1e-6
NEG = -30000.0
NW = 4


class Res:
    __slots__ = ("name", "w", "r")

    def __init__(self, name):
        self.name = name
        self.w = None
        self.r = {}


class DSem:
    __slots__ = ("key", "count")

    def __init__(self, key):
        self.key = key
        self.count = 0


class Prog:
    ENGS = ("pe", "act", "dve", "pool", "sp")

    def __init__(self, nc, es):
        self.nc = nc
        self.es = es
        self.sems = {}
        self.items = {e: [] for e in self.ENGS}
        self.n = {e: 0 for e in self.ENGS}
        self.waited = {e: {} for e in self.ENGS}
        for e in self.ENGS:
            self.sems["eng_" + e] = es.enter_context(nc.semaphore("sem_" + e))

    def dsem(self, name):
        key = "d_" + name
        self.sems[key] = self.es.enter_context(self.nc.semaphore("dsem_" + name))
        return DSem(key)

    def _deps(self, eng, reads, writes):
        deps = {}

        def add(tok):
            if tok is None:
                return
            k, c = tok
            if k == "eng_pe" and eng == "pe":
                return
            if deps.get(k, 0) < c:
                deps[k] = c
        for r in reads:
            add(r.w)
        for w in writes:
            add(w.w)
            for k, c in w.r.items():
                add((k, c))
        waits = []
        wd = self.waited[eng]
        for k, c in deps.items():
            if wd.get(k, 0) >= c:
                continue
            wd[k] = c
            waits.append((k, c))
        return waits

    def _commit(self, tok, reads, writes):
        k, c = tok
        for r in reads:
            if r.r.get(k, 0) < c:
                r.r[k] = c
        for w in writes:
            w.w = tok
            w.r = {}

    def op(self, eng, fn, reads=(), writes=()):
        waits = self._deps(eng, reads, writes)
        self.n[eng] += 1
        tok = ("eng_" + eng, self.n[eng])
        self.items[eng].append((waits, fn, "eng_" + eng, 1))
        self._commit(tok, reads, writes)
        return tok

    def dma(self, eng, fn, dsem, reads=(), writes=()):
        waits = self._deps(eng, reads, writes)
        dsem.count += 16
        tok = (dsem.key, dsem.count)
        self.items[eng].append((waits, fn, dsem.key, 16))
        self._commit(tok, reads, writes)
        return tok

    def wait_all(self, eng, toks):
        self.items[eng].append((list(toks), None, None, 0))

    def replay(self, eng, e):
        sems = self.sems
        for waits, fn, skey, inc in self.items[eng]:
            for k, c in waits:
                e.wait_ge(sems[k], c)
            if fn is None:
                continue
            ins = fn(e)
            ins.then_inc(sems[skey], inc)

    def run(self):
        block = self.es.enter_context(self.nc.Block())
        P = self

        @block.tensor
        def _(e):
            P.replay("pe", e)

        @block.scalar
        def _(e):
            P.replay("act", e)

        @block.vector
        def _(e):
            P.replay("dve", e)

        @block.gpsimd
        def _(e):
            P.replay("pool", e)

        @block.sync
        def _(e):
            P.replay("sp", e)


class Buf:
    __slots__ = ("ap", "res")

    def __init__(self, ap, res):
        self.ap = ap
        self.res = res


class Arena:
    def __init__(self, nc, es, name, nbytes, seg=512):
        self.t = es.enter_context(nc.sbuf_tensor(name, [128, nbytes // 4], F32))
        self.seg = seg
        self.nbytes = nbytes
        self.res = [Res("%s_%d" % (name, i)) for i in range((nbytes + seg - 1) // seg)]

    def buf(self, off, nbytes, dtype=F32):
        assert off % 4 == 0 and nbytes % 4 == 0 and off + nbytes <= self.nbytes, (off, nbytes)
        ap = self.t[:, off // 4:(off + nbytes) // 4]
        if dtype == BF16:
            ap = ap.bitcast(BF16)
        s0 = off // self.seg
        s1 = (off + nbytes - 1) // self.seg
        return Buf(ap, self.res[s0:s1 + 1])


def build_program(debug=False, nown=NOWN, nch0=NCH):
    nc = bass.Bass("TRN2", target_bir_lowering=False)
    dbg_out = {}

    def din(name, shape, dt=F32):
        return nc.dram_tensor(name, shape, dt, kind="ExternalInput").ap()

    x_all = din("x_all", [S, D])
    x_own = din("x_own", [NOWN, CH, D])
    x_halo = din("x_halo", [NOWN, 32, D])
    w_in = din("w_in", [D, 2560])
    w_out = din("w_out", [D, D])
    w_gate = din("w_gate", [D, DFF])
    w_up = din("w_up", [D, DFF])
    w_down = din("w_down", [DFF, D])
    gT_d = din("gT", [128, 16])
    gpost_d = din("gpost", [2, D])
    pvec_d = din("pvec", [128, 140])
    bb_d = din("bb", [128, 8])
    cbf_d = din("cbf", [128, 1408])
    cf_d = din("cf", [128, 256])
    out_own = nc.dram_tensor("out_own", [NOWN, CH, D], F32, kind="ExternalOutput").ap()

    w_in_v = w_in.rearrange("(k p) n -> p k n", p=128)
    w_out_v = w_out.rearrange("(k p) n -> p k n", p=128)
    w_gate_v = w_gate.rearrange("(k p) n -> p k n", p=128)
    w_up_v = w_up.rearrange("(k p) n -> p k n", p=128)
    w_down_v = w_down.rearrange("(k p) n -> p k n", p=128)

    with ExitStack() as es:
        E = es.enter_context
        P = Prog(nc, es)
        KT = E(nc.sbuf_tensor("KT", [128, 4, S], BF16))
        V = E(nc.sbuf_tensor("V", [128, 64, 512], BF16))
        KTres = [[Res("KT%d_%d" % (i, h)) for h in range(4)] for i in range(NCH)]
        Vres = [[Res("V%d_%d" % (i, h)) for h in range(4)] for i in range(NCH)]
        R1 = Arena(nc, es, "R1", 24576)
        R2 = Arena(nc, es, "R2", 22528)
        wring = [E(nc.sbuf_tensor("wring%d" % i, [128, 2048], BF16)) for i in range(NW)]
        wres = [Res("wring%d" % i) for i in range(NW)]
        wsem = [P.dsem("w%d" % i) for i in range(NW)]
        gpost = E(nc.sbuf_tensor("gpost_sb", [128, D], F32))
        gpost_res = Res("gpost")
        gpost_sem = P.dsem("gpost")
        sg = E(nc.sbuf_tensor("sg", [128, 2, 512], F32))
        sg_res = [Res("sg0"), Res("sg1")]
        cbf = E(nc.sbuf_tensor("cbf_sb", [128, 1408], BF16))
        cf = E(nc.sbuf_tensor("cf_sb", [128, 256], F32))
        pv = E(nc.sbuf_tensor("pv_sb", [128, 140], F32))
        gTp = E(nc.sbuf_tensor("gTp", [128, 16], F32))
        bbt = E(nc.sbuf_tensor("bbt", [128, 8], F32))
        const_res = Res("consts")
        NST = 12
        st = E(nc.sbuf_tensor("st", [128, 4 * NST], F32))
        st_res = [Res("st%d" % i) for i in range(NST)]
        st_ctr = [0]
        pbig = [E(nc.psum_tensor("pbig%d" % i, [128, 1024], F32)) for i in range(4)]
        pb = [pbig[k // 2][:, (k % 2) * 512:(k % 2 + 1) * 512] for k in range(8)]
        pr = [Res("pb%d" % i) for i in range(8)]

        ident = cbf[:, 0:128]
        triNeg = cbf[:, 128:256]
        restNeg = cbf[:, 256:384]
        flagI = cbf[:, 384:512]
        Mm = cbf[:, 512:1408]
        ones_f = cf[:, 0:128]
        blockones_f = cf[:, 128:256]
        convw = pv[:, 0:124].rearrange("p (c w) -> p c w", c=4)
        conv_b = pv[:, 124:128]
        ln_g = pv[:, 128:132]
        ln_b = pv[:, 132:136]
        g_attn = pv[:, 136:140]
        gT_mix = gTp[:, 0:8]
        gT_ffn = gTp[:, 8:16]

        csem_sw = P.dsem("consts_sw")
        csem_hw = P.dsem("consts_hw")
        const_res_hw = Res("consts_hw")
        P.dma("pool", lambda e: e.dma_start(out=cbf[:], in_=cbf_d), csem_sw, writes=[const_res])
        P.dma("sp", lambda e: e.dma_start(out=cf[:], in_=cf_d), csem_hw, writes=[const_res_hw])
        P.dma("sp", lambda e: e.dma_start(out=pv[:], in_=pvec_d), csem_hw, writes=[const_res_hw])
        P.dma("sp", lambda e: e.dma_start(out=gTp[:], in_=gT_d), csem_hw, writes=[const_res_hw])
        P.dma("sp", lambda e: e.dma_start(out=bbt[:], in_=bb_d), csem_hw, writes=[const_res_hw])
        const_res.w = (csem_sw.key, csem_sw.count)
        const_res_hw.w = (csem_hw.key, csem_hw.count)
        CR = [const_res, const_res_hw]

        def new_st():
            i = st_ctr[0] % NST
            st_ctr[0] += 1
            return st[:, 4 * i:4 * i + 4], st_res[i]

        def rstd_from(stt, sres, scale):
            P.op("act", lambda e: e.activation(out=stt[:, 2:3], in_=stt[:, 0:1], func=AF.Ln, scale=scale, bias=EPS),
                 writes=[sres])
            P.op("act", lambda e: e.activation(out=stt[:, 3:4], in_=stt[:, 2:3], func=AF.Exp, scale=-0.5),
                 writes=[sres])

        units = []
        for j in range(nown):
            for c0 in (0, 512, 256, 768, 1024, 1280):
                units.append(("win", c0))
            for c0 in (0, 256, 512, 768):
                units.append(("wout", c0))
            for k in range(11):
                units.append(("wg", 256 * k))
                units.append(("wu", 256 * k))
            for k in range(11):
                units.append(("wd", k))
        w_issued = [0]
        w_used = [0]

        def w_issue_upto(n):
            while w_issued[0] < min(n, len(units)):
                u = w_issued[0]
                kind, a = units[u]
                slot = u % NW
                if kind == "wd":
                    dst = wring[slot][:].rearrange("p (s n) -> p s n", s=2)
                    src = w_down_v[:, 2 * a:2 * a + 2, :]
                else:
                    dst = wring[slot][:].rearrange("p (k n) -> p k n", k=8)
                    srcv = {"win": w_in_v, "wout": w_out_v, "wg": w_gate_v, "wu": w_up_v}[kind]
                    src = srcv[:, :, a:a + 256]
                P.dma("pool", (lambda d, s: (lambda e: e.dma_start(out=d, in_=s)))(dst, src), wsem[slot], writes=[wres[slot]])
                w_issued[0] += 1

        def w_done():
            w_issue_upto(w_used[0] + NW)

        def w_get(kind, a):
            u = w_used[0]
            assert units[u] == (kind, a), (units[u], kind, a)
            w_issue_upto(u + 1)
            w_used[0] += 1
            slot = u % NW
            if kind == "wd":
                return wring[slot][:].rearrange("p (s n) -> p s n", s=2), wres[slot]
            return wring[slot][:].rearrange("p (k n) -> p k n", k=8), wres[slot]

        xs_sems = [P.dsem("xs0"), P.dsem("xs1")]

        nt_ctr = [0]

        def nt_stages(np_, xsb, xnb, junkb, tbank, gT, dst_ap, dst_res):
            stt, sres = new_st()
            tb = pb[tbank][:, :].bitcast(BF16)

            def stats():
                P.op("act", lambda e: e.activation(out=junkb.ap[0:np_, :], in_=xsb.ap[0:np_, :], func=AF.Square,
                                                   accum_out=stt[0:np_, 0:1]),
                     reads=list(xsb.res), writes=list(junkb.res) + [sres])
                rstd_from(stt[0:np_, :], sres, 1.0 / D)

            nt_ctr[0] += 1
            on_act = (nt_ctr[0] % 2 == 0) and np_ == 128

            def scale():
                if on_act:
                    P.op("act", lambda e: e.activation(out=xnb.ap[0:np_, :], in_=xsb.ap[0:np_, :], func=AF.Identity,
                                                       scale=stt[0:np_, 3:4]),
                         reads=list(xsb.res) + [sres], writes=list(xnb.res))
                else:
                    P.op("dve", lambda e: e.tensor_scalar(out=xnb.ap[0:np_, :], in0=xsb.ap[0:np_, :], scalar1=stt[0:np_, 3:4],
                                                          scalar2=None, op0=ALU.mult),
                         reads=list(xsb.res) + [sres], writes=list(xnb.res))

            def transpose():
                def tr(e):
                    ins = None
                    for k in range(8):
                        ins = e.transpose(tb[:, k * np_:(k + 1) * np_], xnb.ap[0:np_, k * 128:(k + 1) * 128],
                                          ident[0:np_, 0:np_])
                    return ins
                P.op("pe", tr, reads=list(xnb.res) + CR, writes=[pr[tbank]])

            def evac():
                P.op("dve", lambda e: e.tensor_tensor(out=dst_ap, in0=tb[:, 0:8 * np_].rearrange("p (k t) -> p k t", k=8),
                                                      in1=gT.unsqueeze(2).broadcast_to([128, 8, np_]), op=ALU.mult),
                     reads=CR, writes=[pr[tbank]] + list(dst_res))
            return [stats, scale, transpose, evac]

        def norm_transpose(np_, xsb, xnb, junkb, tbank, gT, dst_ap, dst_res):
            for f in nt_stages(np_, xsb, xnb, junkb, tbank, gT, dst_ap, dst_res):
                f()

        def emit_pipelines(pipes):
            T = max(st + nb + len(stg) - 1 for nb, stg, st in pipes)
            for t in range(T):
                for nb, stg, st in pipes:
                    for si in range(len(stg) - 1, -1, -1):
                        blk = t - st - si
                        if 0 <= blk < nb:
                            stg[si](blk)

        def mm_group(e, out, pairs, first=True, last=True):
            n = len(pairs)
            ins = None
            for i, (l, r) in enumerate(pairs):
                ins = e.matmul(out, lhsT=l, rhs=r, start=(first and i == 0), stop=(last and i == n - 1))
            return ins

        wkv = R2.buf(0, 16384, BF16)
        wkv_v = wkv.ap.rearrange("p (k n) -> p k n", k=8)
        wkv_sem = P.dsem("wkv")
        for hf in range(2):
            P.dma("pool", (lambda hf: (lambda e: e.dma_start(out=wkv_v[:, :, hf * 512:(hf + 1) * 512],
                                                             in_=w_in_v[:, :, 1536 + hf * 512:1536 + (hf + 1) * 512])))(hf),
                  wkv_sem, writes=list(wkv.res))
        for r in wkv.res:
            r.w = (wkv_sem.key, wkv_sem.count)
        aT0 = [R1.buf(0, 8192, BF16), R1.buf(8192, 8192, BF16)]
        xs0 = [R1.buf(16384, 4096, F32), R1.buf(20480, 4096, F32)]
        xn0 = [R2.buf(16384, 2048, BF16), R2.buf(18432, 2048, BF16)]
        junk0 = R2.buf(20480, 2048, BF16)
        p0 = {"obank": 2, "evac": 0}
        p0_nts = {}

        def p0_get(g):
            if g not in p0_nts:
                ci, bi = g // 4, g % 4
                aT = aT0[ci % 2]
                aT_v = aT.ap.rearrange("p (k t) -> p k t", k=8)
                sl = g % 2
                p0_nts[g] = nt_stages(128, xs0[sl], xn0[sl], junk0, g % 2, gT_mix, aT_v[:, :, bi * 128:(bi + 1) * 128], aT.res)
            return p0_nts[g]

        def p0_load(g):
            sl = g % 2
            r0 = g * 128
            P.dma("sp", (lambda sl, r0: (lambda e: e.dma_start(out=xs0[sl].ap, in_=x_all[r0:r0 + 128, :])))(sl, r0),
                  xs_sems[sl], writes=list(xs0[sl].res))

        def p0_evac(bk, dst, res):
            if p0["evac"] % 2 == 0:
                P.op("act", (lambda bk, dst: (lambda e: e.activation(out=dst, in_=pb[bk][:, :], func=AF.Identity)))(bk, dst),
                     writes=[pr[bk], res])
            else:
                P.op("dve", (lambda bk, dst: (lambda e: e.tensor_copy(out=dst, in_=pb[bk][:, :])))(bk, dst),
                     writes=[pr[bk], res])
            p0["evac"] += 1

        def p0_mm_group(ci, gi):
            aT = aT0[ci % 2]
            aT_v = aT.ap.rearrange("p (k t) -> p k t", k=8)
            bk = p0["obank"]
            p0["obank"] = 2 + (bk - 2 + 1) % 6
            if gi < 4:
                hp = gi
                P.op("pe", (lambda bk, hp, aT_v: (lambda e: mm_group(
                    e, pb[bk][:, :], [(wkv_v[:, kc, hp * 128:(hp + 1) * 128], aT_v[:, kc, :]) for kc in range(8)])))(bk, hp, aT_v),
                    reads=list(aT.res) + list(wkv.res), writes=[pr[bk]])
                p0_evac(bk, KT[:, hp, ci * CH:(ci + 1) * CH], KTres[ci][hp])
            else:
                bi = gi - 4
                P.op("pe", (lambda bk, bi, aT_v: (lambda e: mm_group(
                    e, pb[bk][:, :], [(aT_v[:, kc, bi * 128:(bi + 1) * 128], wkv_v[:, kc, 512:1024]) for kc in range(8)])))(bk, bi, aT_v),
                    reads=list(aT.res) + list(wkv.res), writes=[pr[bk]])
                p0_evac(bk, V[:, ci * 4 + bi, :], Vres[ci][bi])

        NG = 4 * nch0
        for t in range(NG + 12):
            for si in (4, 3, 2, 1):
                g = t - si
                if 0 <= g < NG:
                    p0_get(g)[si - 1]()
            if t < NG:
                p0_load(t)
            c = (t - 8) // 4
            if t >= 8 and c < nch0:
                for gi in ((t - 8) % 4 * 2, (t - 8) % 4 * 2 + 1):
                    p0_mm_group(c, gi)

        aT1 = R1.buf(16384, 8192, BF16)
        aT1_v = aT1.ap.rearrange("p (k t) -> p k t", k=8)
        xs1 = [R2.buf(0, 4096, F32), R2.buf(4096, 4096, F32)]
        xn1 = [R2.buf(8192, 2048, BF16), R2.buf(10240, 2048, BF16)]
        junk1 = R2.buf(12288, 2048, BF16)
        aTh = R2.buf(20992, 512, BF16)
        aTh_v = aTh.ap.rearrange("p (k t) -> p k t", k=8)
        yT = [R1.buf(ct * 2048, 2048, F32) for ct in range(4)]
        ysq = [R1.buf(8192 + ct * 2048, 2048, F32) for ct in range(4)]
        diagA = R1.buf(16384, 4096, BF16)
        diagA_v = diagA.ap.rearrange("p (w c) -> p w c", w=16)
        diagB = R1.buf(20480, 3840, BF16)
        diagB_v = diagB.ap.rearrange("p (w c) -> p w c", w=15)
        e_ring = [R1.buf(i * 4096, 4096, F32) for i in range(3)]
        sp_ring = [R1.buf(12288 + i * 2048, 2048, BF16) for i in range(3)]
        ec_ring = [R1.buf(18432 + i * 2048, 2048, BF16) for i in range(2)]
        osb = R1.buf(22528, 2048, F32)
        hb = [R1.buf(i * 4096, 4096, F32) for i in range(4)]
        fT = R1.buf(16384, 8192, BF16)
        fT_v = fT.ap.rearrange("p (k t) -> p k t", k=8)
        stage = [R1.buf(16384, 4096, F32), R1.buf(20480, 4096, F32)]
        mixedT = [R2.buf(ct * 1024, 1024, BF16) for ct in range(8)]
        QT = [R2.buf(8192 + hp * 1024, 1024, BF16) for hp in range(4)]
        gluT = [R2.buf(12288 + ct * 1088, 1088, BF16) for ct in range(4)]
        sigS = [R2.buf(16640, 2176, F32), R2.buf(18816, 2176, F32)]
        junk2b = R2.buf(16384, 2048, BF16)
        at_ring = [R2.buf(12288 + i * 2048, 2048, BF16) for i in range(2)]
        sqb = R2.buf(16384, 2048, F32)
        rsb = R2.buf(18432, 2048, F32)
        hn = [R2.buf(0, 2048, BF16), R2.buf(2048, 2048, BF16)]
        xs2 = [R2.buf(8192, 4096, F32), R2.buf(12288, 4096, F32)]
        junk3 = R2.buf(4096, 2048, BF16)
        actT = [R2.buf(ft * 1024, 1024, BF16) for ft in range(NFT)]
        junk4b = R2.buf(14336, 2048, BF16)
        out_sems = [P.dsem("out%d" % i) for i in range(4)]
        dbg_sem = P.dsem("dbg") if debug else None

        def dump(name, ap, res, dt=BF16):
            if not debug:
                return
            shp = list(ap.shape)
            d = nc.dram_tensor("dbg_" + name, shp, dt, kind="ExternalOutput").ap()
            dbg_out[name] = d
            P.dma("sp", (lambda d, ap: (lambda e: e.dma_start(out=d, in_=ap)))(d, ap), dbg_sem, reads=list(res))
        out_toks = {}
        xblk = 0

        def phase_B_pipe(j):
            nts = {}

            def get(i):
                if i not in nts:
                    sl = i % 2
                    if i == 0:
                        nts[i] = nt_stages(32, xs1[sl], xn1[sl], junk1, 0, gT_mix, aTh_v[:, :, :], aTh.res)
                    else:
                        bi = i - 1
                        nts[i] = nt_stages(128, xs1[sl], xn1[sl], junk1, bi % 2, gT_mix, aT1_v[:, :, bi * 128:(bi + 1) * 128], aT1.res)
                return nts[i]

            def load(i):
                sl = i % 2
                if i == 0:
                    P.dma("sp", (lambda sl: (lambda e: e.dma_start(out=xs1[sl].ap[0:32, :], in_=x_halo[j])))(sl),
                          xs_sems[sl], writes=list(xs1[sl].res))
                else:
                    bi = i - 1
                    P.dma("sp", (lambda sl, bi: (lambda e: e.dma_start(out=xs1[sl].ap, in_=x_own[j, bi * 128:(bi + 1) * 128, :])))(sl, bi),
                          xs_sems[sl], writes=list(xs1[sl].res))
            return (5, [load, lambda i: get(i)[0](), lambda i: get(i)[1](), lambda i: get(i)[2](), lambda i: get(i)[3]()], 0)

        w_done()
        emit_pipelines([phase_B_pipe(0)])

        for j in range(nown):
            if j == 0:
                dump("aT", aT1.ap, aT1.res)
                dump("aTh", aTh.ap, aTh.res)
                dump("KT", KT[:, :, 0:1024], [r for ci in range(2) for r in KTres[ci]])
                dump("V", V[:, 0:8, :], [r for ci in range(2) for r in Vres[ci]])
            for half in range(2):
                if half == 1:
                    w_done()
                wv, wvr = w_get("win", 0 + 256 * half)
                wg_, wgr = w_get("win", 512 + 256 * half)
                for sub in range(2):
                    ct = half * 2 + sub
                    bs = (2, 3, 4) if ct % 2 == 0 else (5, 6, 7)
                    for (wt, wr, bmain, hoff) in ((wv, wvr, bs[0], 0), (wg_, wgr, bs[1], 32)):
                        P.op("pe", (lambda wt, sub, bmain, hb_, hoff: (lambda e: (
                            mm_group(e, pb[bmain][:, :], [(wt[:, kc, sub * 128:(sub + 1) * 128], aT1_v[:, kc, :]) for kc in range(8)]),
                            mm_group(e, pb[hb_][:, hoff:hoff + 32], [(wt[:, kc, sub * 128:(sub + 1) * 128], aTh_v[:, kc, :]) for kc in range(8)]))[1]))(
                                wt, sub, bmain, bs[2], hoff),
                            reads=[wr] + list(aT1.res) + list(aTh.res), writes=[pr[bmain], pr[bs[2]]])
                    sS = sigS[ct % 2]
                    P.op("act", (lambda sS, b: (lambda e: e.activation(out=sS.ap[:, 32:544], in_=pb[b][:, :], func=AF.Sigmoid)))(sS, bs[1]),
                         writes=[pr[bs[1]]] + list(sS.res))
                    P.op("act", (lambda sS, b: (lambda e: e.activation(out=sS.ap[:, 0:32], in_=pb[b][:, 32:64], func=AF.Sigmoid)))(sS, bs[2]),
                         writes=[pr[bs[2]]] + list(sS.res))
                    P.op("dve", (lambda sS, b, ct: (lambda e: e.tensor_tensor(out=gluT[ct].ap[:, 32:544], in0=pb[b][:, :], in1=sS.ap[:, 32:544], op=ALU.mult)))(sS, bs[0], ct),
                         reads=list(sS.res), writes=[pr[bs[0]]] + list(gluT[ct].res))
                    P.op("dve", (lambda sS, b, ct: (lambda e: e.tensor_tensor(out=gluT[ct].ap[:, 0:32], in0=pb[b][:, 0:32], in1=sS.ap[:, 0:32], op=ALU.mult)))(sS, bs[2], ct),
                         reads=list(sS.res), writes=[pr[bs[2]]] + list(gluT[ct].res))
            w_done()
            for half in range(2):
                if half == 1:
                    w_done()
                wq, wqr = w_get("win", 1024 + 256 * half)
                for sub in range(2):
                    hp = half * 2 + sub
                    bk = 2 + hp
                    P.op("pe", (lambda wq, sub, bk: (lambda e: mm_group(
                        e, pb[bk][:, :], [(wq[:, kc, sub * 128:(sub + 1) * 128], aT1_v[:, kc, :]) for kc in range(8)])))(wq, sub, bk),
                        reads=[wqr] + list(aT1.res), writes=[pr[bk]])
                    P.op("dve", (lambda bk, hp: (lambda e: e.tensor_scalar(out=QT[hp].ap, in0=pb[bk][:, :], scalar1=0.125, scalar2=None, op0=ALU.mult)))(bk, hp),
                         writes=[pr[bk]] + list(QT[hp].res))

            w_done()
            if j == 0:
                for ct in range(4):
                    dump("glu%d" % ct, gluT[ct].ap, gluT[ct].res)
                    dump("QT%d" % ct, QT[ct].ap, QT[ct].res)
            for ct in range(4):
                bk = 2 + ct
                for (dbuf, dv, w0, nw) in ((diagA, diagA_v, 0, 16), (diagB, diagB_v, 16, 15)):
                    P.op("dve", (lambda dv, w0, nw, ct: (lambda e: e.tensor_tensor(
                        out=dv, in0=ident.unsqueeze(1).broadcast_to([128, nw, 128]),
                        in1=convw[:, ct, w0:w0 + nw].unsqueeze(2).broadcast_to([128, nw, 128]), op=ALU.mult)))(dv, w0, nw, ct),
                        reads=CR, writes=list(dbuf.res))
                    P.op("pe", (lambda bk, ct, dv, w0, nw: (lambda e: mm_group(
                        e, pb[bk][:, :], [(dv[:, w - w0, :], gluT[ct].ap[:, w + 2:w + 2 + 512]) for w in range(w0, w0 + nw)],
                        first=(w0 == 0), last=(w0 + nw == 31))))(bk, ct, dv, w0, nw),
                        reads=list(dbuf.res) + list(gluT[ct].res), writes=[pr[bk]])
                P.op("act", (lambda bk, ct: (lambda e: e.activation(out=yT[ct].ap, in_=pb[bk][:, :], func=AF.Identity, bias=conv_b[:, ct:ct + 1])))(bk, ct),
                     reads=CR, writes=[pr[bk]] + list(yT[ct].res))
                P.op("act", (lambda ct: (lambda e: e.activation(out=ysq[ct].ap, in_=yT[ct].ap, func=AF.Square)))(ct),
                     reads=list(yT[ct].res), writes=list(ysq[ct].res))
            P.op("pe", lambda e: mm_group(e, pb[6][:, :], [(ones_f, yT[ct].ap) for ct in range(4)]),
                 reads=CR + [r for ct in range(4) for r in yT[ct].res], writes=[pr[6]])
            P.op("pe", lambda e: mm_group(e, pb[7][:, :], [(ones_f, ysq[ct].ap) for ct in range(4)]),
                 reads=CR + [r for ct in range(4) for r in ysq[ct].res], writes=[pr[7]])
            meanb, msqb, varb = ysq[0], ysq[1], ysq[2]
            P.op("dve", lambda e: e.tensor_scalar(out=meanb.ap, in0=pb[6][:, :], scalar1=1.0 / 512, scalar2=None, op0=ALU.mult),
                 writes=[pr[6]] + list(meanb.res))
            P.op("dve", lambda e: e.tensor_tensor(out=msqb.ap, in0=meanb.ap, in1=meanb.ap, op=ALU.mult),
                 reads=list(meanb.res), writes=list(msqb.res))
            P.op("dve", lambda e: e.scalar_tensor_tensor(out=varb.ap, in0=pb[7][:, :], scalar=1.0 / 512, in1=msqb.ap, op0=ALU.mult, op1=ALU.subtract),
                 reads=list(msqb.res), writes=[pr[7]] + list(varb.res))
            P.op("act", lambda e: e.activation(out=varb.ap, in_=varb.ap, func=AF.Ln, bias=EPS), writes=list(varb.res))
            P.op("act", lambda e: e.activation(out=varb.ap, in_=varb.ap, func=AF.Exp, scale=-0.5), writes=list(varb.res))
            for ct in range(4):
                P.op("dve", (lambda ct: (lambda e: e.tensor_tensor(out=yT[ct].ap, in0=yT[ct].ap, in1=meanb.ap, op=ALU.subtract)))(ct),
                     reads=list(meanb.res), writes=list(yT[ct].res))
                P.op("dve", (lambda ct: (lambda e: e.tensor_tensor(out=yT[ct].ap, in0=yT[ct].ap, in1=varb.ap, op=ALU.mult)))(ct),
                     reads=list(varb.res), writes=list(yT[ct].res))
                P.op("act", (lambda ct: (lambda e: e.activation(out=mixedT[ct].ap, in_=yT[ct].ap, func=AF.Silu,
                                                                  scale=ln_g[:, ct:ct + 1], bias=ln_b[:, ct:ct + 1])))(ct),
                     reads=CR + list(yT[ct].res), writes=list(mixedT[ct].res))

            nkb = 8 * (j + 1)
            items = []
            for hp in range(4):
                for kb in range(nkb - 1, -1, -1):
                    items.append((hp, kb, kb == nkb - 1, kb == 0))
            N = len(items)
            deferred = {}
            ZSL = [(0, 1), (2, 3)]
            CB = (4, 5)
            OB = 6
            SB = 7

            def q0_of(i):
                kbl = items[i][1] - 8 * j
                return 128 * (kbl - 4) if kbl >= 4 else 0

            def v3(ap2d, q0):
                return ap2d.rearrange("p (a n) -> p a n", a=2)[:, :, q0:512]

            def Zop(i):
                hp, kb, first, last = items[i]
                za, zb = ZSL[i % 2]
                kbl = kb - 8 * j
                q0 = q0_of(i)

                def fn(e, hp=hp, kb=kb, kbl=kbl, za=za, zb=zb, q0=q0):
                    ins = None
                    for hl, bk in ((0, za), (1, zb)):
                        p0_ = 64 * hl
                        ins = e.matmul(pb[bk][:, q0:512], lhsT=KT[p0_:p0_ + 64, hp, kb * 128:(kb + 1) * 128], rhs=QT[hp].ap[p0_:p0_ + 64, q0:512],
                                       start=True, stop=(kbl < 0))
                    if kbl >= 0:
                        o = 384 - 128 * (kbl % 4)
                        for bk in (za, zb):
                            ins = e.matmul(pb[bk][:, q0:512], lhsT=(flagI if kbl < 4 else ident), rhs=Mm[:, o + q0:o + 512], start=False, stop=True)
                    return ins
                P.op("pe", fn, reads=[KTres[kb // 4][hp]] + list(QT[hp].res) + CR, writes=[pr[za], pr[zb]])

            def Ezop(i):
                hp, kb, first, last = items[i]
                za, zb = ZSL[i % 2]
                kbl = kb - 8 * j
                q0 = q0_of(i)
                eb = e_ring[i % 3]
                bias = bbt[:, kbl:kbl + 1] if kbl >= 0 else 0.0
                if q0 == 0:
                    P.op("act", (lambda za, eb, bias: (lambda e: e.activation(out=eb.ap, in_=pbig[za // 2][:, :], func=AF.Exp, bias=bias)))(za, eb, bias),
                         reads=CR, writes=[pr[za], pr[zb]] + list(eb.res))
                else:
                    def fn(e, za=za, zb=zb, eb=eb, bias=bias, q0=q0):
                        e.activation(out=eb.ap[:, q0:512], in_=pb[za][:, q0:512], func=AF.Exp, bias=bias)
                        return e.activation(out=eb.ap[:, 512 + q0:1024], in_=pb[zb][:, q0:512], func=AF.Exp, bias=bias)
                    P.op("act", fn, reads=CR, writes=[pr[za], pr[zb]] + list(eb.res))

            def Lnop(i):
                eb = e_ring[i % 3]
                sb = sp_ring[i % 3]
                q0 = q0_of(i)
                if q0 == 0:
                    P.op("act", (lambda eb, sb: (lambda e: e.activation(out=sb.ap, in_=eb.ap, func=AF.Ln, bias=1.0)))(eb, sb),
                         reads=list(eb.res), writes=list(sb.res))
                else:
                    def fn(e, eb=eb, sb=sb, q0=q0):
                        e.activation(out=sb.ap[:, q0:512], in_=eb.ap[:, q0:512], func=AF.Ln, bias=1.0)
                        return e.activation(out=sb.ap[:, 512 + q0:1024], in_=eb.ap[:, 512 + q0:1024], func=AF.Ln, bias=1.0)
                    P.op("act", fn, reads=list(eb.res), writes=list(sb.res))

            def Triop(i):
                first = items[i][2]
                sb = sp_ring[i % 3]
                q0 = q0_of(i)

                def fn(e, sb=sb, first=first, q0=q0):
                    e.matmul(pb[CB[0]][:, q0:512], lhsT=triNeg, rhs=sb.ap[:, q0:512], start=first, stop=True, skip_group_check=True)
                    return e.matmul(pb[CB[1]][:, q0:512], lhsT=triNeg, rhs=sb.ap[:, 512 + q0:1024], start=first, stop=True, skip_group_check=True)
                P.op("pe", fn, reads=list(sb.res) + CR, writes=[pr[CB[0]], pr[CB[1]]])

            def ECop(i):
                xb = ec_ring[i % 2]
                q0 = q0_of(i)
                if q0 == 0:
                    P.op("act", (lambda xb: (lambda e: e.activation(out=xb.ap, in_=pbig[CB[0] // 2][:, :], func=AF.Exp)))(xb),
                         writes=[pr[CB[0]], pr[CB[1]]] + list(xb.res))
                else:
                    def fn(e, xb=xb, q0=q0):
                        e.activation(out=xb.ap[:, q0:512], in_=pb[CB[0]][:, q0:512], func=AF.Exp)
                        return e.activation(out=xb.ap[:, 512 + q0:1024], in_=pb[CB[1]][:, q0:512], func=AF.Exp)
                    P.op("act", fn, writes=[pr[CB[0]], pr[CB[1]]] + list(xb.res))

            def Restop(i):
                sb = sp_ring[i % 3]
                q0 = q0_of(i)

                def fn(e, sb=sb, q0=q0):
                    e.matmul(pb[CB[0]][:, q0:512], lhsT=restNeg, rhs=sb.ap[:, q0:512], start=False, stop=True, skip_group_check=True)
                    return e.matmul(pb[CB[1]][:, q0:512], lhsT=restNeg, rhs=sb.ap[:, 512 + q0:1024], start=False, stop=True, skip_group_check=True)
                P.op("pe", fn, reads=list(sb.res) + CR, writes=[pr[CB[0]], pr[CB[1]]])

            def Mulop(i):
                eb = e_ring[i % 3]
                xb = ec_ring[i % 2]
                ab = at_ring[i % 2]
                q0 = q0_of(i)
                if q0 == 0:
                    P.op("dve", (lambda eb, xb, ab: (lambda e: e.tensor_tensor(out=ab.ap, in0=eb.ap, in1=xb.ap, op=ALU.mult)))(eb, xb, ab),
                         reads=list(eb.res) + list(xb.res), writes=list(ab.res))
                else:
                    def fn(e, eb=eb, xb=xb, ab=ab, q0=q0):
                        e.tensor_tensor(out=ab.ap[:, q0:512], in0=eb.ap[:, q0:512], in1=xb.ap[:, q0:512], op=ALU.mult)
                        return e.tensor_tensor(out=ab.ap[:, 512 + q0:1024], in0=eb.ap[:, 512 + q0:1024], in1=xb.ap[:, 512 + q0:1024], op=ALU.mult)
                    P.op("dve", fn, reads=list(eb.res) + list(xb.res), writes=list(ab.res))

            def AVop(i):
                hp, kb, first, last = items[i]
                ab = at_ring[i % 2]
                c0 = hp * 128
                q0 = q0_of(i)

                def fn(e, kb=kb, c0=c0, ab=ab, first=first, last=last, q0=q0):
                    e.matmul(pb[OB][0:64, q0:512], lhsT=V[:, kb, c0:c0 + 64], rhs=ab.ap[:, q0:512], start=first, stop=last, skip_group_check=True)
                    return e.matmul(pb[OB][64:128, q0:512], lhsT=V[:, kb, c0 + 64:c0 + 128], rhs=ab.ap[:, 512 + q0:1024], start=first, stop=last,
                                    skip_group_check=True)
                P.op("pe", fn, reads=[Vres[kb // 4][kb % 4]] + list(ab.res), writes=[pr[OB]])
                if last:
                    P.op("dve", lambda e: e.tensor_copy(out=osb.ap, in_=pb[OB][:, :]), writes=[pr[OB]] + list(osb.res))
                    deferred.setdefault(min(i + 2, N), []).append(lambda hp=hp: headnorm1(hp))
                    deferred.setdefault(min(i + 3, N), []).append(lambda hp=hp: headnorm2(hp))

            def headnorm1(hp):
                P.op("dve", lambda e: e.tensor_tensor(out=sqb.ap, in0=osb.ap, in1=osb.ap, op=ALU.mult),
                     reads=list(osb.res), writes=list(sqb.res))
                P.op("pe", lambda e: e.matmul(pb[SB][:, :], lhsT=blockones_f, rhs=sqb.ap, start=True, stop=True),
                     reads=list(sqb.res) + CR, writes=[pr[SB]])

            def headnorm2(hp):
                P.op("act", lambda e: e.activation(out=rsb.ap, in_=pb[SB][:, :], func=AF.Ln, scale=1.0 / 64, bias=EPS),
                     writes=[pr[SB]] + list(rsb.res))
                P.op("act", lambda e: e.activation(out=rsb.ap, in_=rsb.ap, func=AF.Exp, scale=-0.5), writes=list(rsb.res))
                P.op("dve", (lambda hp: (lambda e: e.scalar_tensor_tensor(out=mixedT[4 + hp].ap, in0=osb.ap, scalar=g_attn[:, hp:hp + 1],
                                                                            in1=rsb.ap, op0=ALU.mult, op1=ALU.mult)))(hp),
                     reads=list(rsb.res) + list(osb.res) + CR, writes=list(mixedT[4 + hp].res))

            Zop(0)
            Zop(1)
            Ezop(0)
            Lnop(0)
            Ezop(1)
            Lnop(1)
            Triop(0)
            for s_ in range(N):
                if s_ + 2 < N:
                    Zop(s_ + 2)
                ECop(s_)
                if not items[s_][3]:
                    Restop(s_)
                if s_ + 1 < N:
                    Triop(s_ + 1)
                if s_ + 2 < N:
                    Ezop(s_ + 2)
                    Lnop(s_ + 2)
                Mulop(s_)
                if s_ >= 1:
                    AVop(s_ - 1)
                for f in deferred.pop(s_, []):
                    f()
            AVop(N - 1)
            for k in sorted(deferred):
                for f in deferred[k]:
                    f()

            if j == 0:
                for ct in range(8):
                    dump("mixed%d" % ct, mixedT[ct].ap, mixedT[ct].res)
            P.dma("sp", lambda e: e.dma_start(out=gpost[:], in_=gpost_d[0:1, :].broadcast_to([128, D])), gpost_sem, writes=[gpost_res])
            wo_units = [w_get("wout", 256 * u) for u in range(4)]

            def OP(blk):
                for u in range(4):
                    wo, wor = wo_units[u]
                    bk = blk * 2 + u // 2
                    c0 = (u % 2) * 256
                    P.op("pe", (lambda wo, blk, bk, c0: (lambda e: mm_group(
                        e, pb[bk][:, c0:c0 + 256], [(mixedT[ct].ap[:, blk * 128:(blk + 1) * 128], wo[:, ct, :]) for ct in range(8)])))(wo, blk, bk, c0),
                        reads=[wor] + [r for ct in range(8) for r in mixedT[ct].res], writes=[pr[bk]])
                if blk == 3:
                    w_done()
            fstate = {}

            def F0(blk, j=j):
                sl = blk % 2
                P.dma("sp", (lambda sl, blk: (lambda e: e.dma_start(out=xs2[sl].ap, in_=x_own[j, blk * 128:(blk + 1) * 128, :])))(sl, blk),
                      xs_sems[sl], writes=list(xs2[sl].res))
                stt, sres = new_st()
                fstate[blk] = (stt, sres)
                P.op("act", (lambda blk, stt: (lambda e: e.activation(out=junk2b.ap, in_=pbig[blk][:, :], func=AF.Square, accum_out=stt[:, 0:1])))(blk, stt),
                     writes=[pr[2 * blk], pr[2 * blk + 1], sres] + list(junk2b.res))

            def F1(blk):
                stt, sres = fstate[blk]
                rstd_from(stt, sres, 1.0 / D)

            def F2(blk):
                stt, sres = fstate[blk]
                sl = blk % 2
                P.op("dve", (lambda blk, stt: (lambda e: e.scalar_tensor_tensor(
                    out=hb[blk].ap, in0=pbig[blk][:, :], scalar=stt[:, 3:4], in1=gpost[:, :], op0=ALU.mult, op1=ALU.mult)))(blk, stt),
                    reads=[sres, gpost_res], writes=[pr[2 * blk], pr[2 * blk + 1]] + list(hb[blk].res))
                P.op("dve", (lambda blk, sl: (lambda e: e.tensor_tensor(out=hb[blk].ap, in0=hb[blk].ap, in1=xs2[sl].ap, op=ALU.add)))(blk, sl),
                     reads=list(xs2[sl].res), writes=list(hb[blk].res))

            gnt = {}

            def G(blk):
                if blk not in gnt:
                    gnt[blk] = nt_stages(128, hb[blk], hn[blk % 2], junk3, blk % 2, gT_ffn, fT_v[:, :, blk * 128:(blk + 1) * 128], fT.res)
                return gnt[blk]
            emit_pipelines([(4, [OP, F0, F1, F2, lambda b: G(b)[0](), lambda b: G(b)[1](), lambda b: G(b)[2](), lambda b: G(b)[3]()], 0)])

            if j == 0:
                for blk in range(4):
                    dump("h%d" % blk, hb[blk].ap, hb[blk].res, F32)

            if j == 0:
                dump("fT", fT.ap, fT.res)
            pair = 0
            for k in range(11):
                wgt, wgr = w_get("wg", 256 * k)
                wut, wur = w_get("wu", 256 * k)
                for sub in range(2):
                    ft = 2 * k + sub
                    gb = (pair % 4) * 2
                    ub = gb + 1
                    ss_ = pair % 2
                    pair += 1
                    P.op("pe", (lambda wgt, sub, gb: (lambda e: mm_group(
                        e, pb[gb][:, :], [(wgt[:, kc, sub * 128:(sub + 1) * 128], fT_v[:, kc, :]) for kc in range(8)])))(wgt, sub, gb),
                        reads=[wgr] + list(fT.res), writes=[pr[gb]])
                    P.op("pe", (lambda wut, sub, ub: (lambda e: mm_group(
                        e, pb[ub][:, :], [(wut[:, kc, sub * 128:(sub + 1) * 128], fT_v[:, kc, :]) for kc in range(8)])))(wut, sub, ub),
                        reads=[wur] + list(fT.res), writes=[pr[ub]])
                    P.op("act", (lambda gb, ss_: (lambda e: e.activation(out=sg[:, ss_, :], in_=pb[gb][:, :], func=AF.Silu)))(gb, ss_),
                         writes=[pr[gb], sg_res[ss_]])
                    P.op("dve", (lambda ub, ss_, ft: (lambda e: e.tensor_tensor(out=actT[ft].ap, in0=pb[ub][:, :], in1=sg[:, ss_, :], op=ALU.mult)))(ub, ss_, ft),
                         reads=[sg_res[ss_]], writes=[pr[ub]] + list(actT[ft].res))
                w_done()

            if j == 0:
                for ft in (0, 1, 21):
                    dump("act%d" % ft, actT[ft].ap, actT[ft].res)
            for k in range(11):
                wdt, wdr = w_get("wd", k)
                for sub in range(2):
                    ft = 2 * k + sub

                    def fn(e, wdt=wdt, sub=sub, ft=ft):
                        ins = None
                        for blk in range(4):
                            for hf in range(2):
                                ins = e.matmul(pb[blk * 2 + hf][:, :], lhsT=actT[ft].ap[:, blk * 128:(blk + 1) * 128],
                                               rhs=wdt[:, sub, hf * 512:(hf + 1) * 512], start=(ft == 0), stop=(ft == NFT - 1))
                        return ins
                    P.op("pe", fn, reads=[wdr] + list(actT[ft].res), writes=list(pr))
                w_done()

            P.dma("sp", lambda e: e.dma_start(out=gpost[:], in_=gpost_d[1:2, :].broadcast_to([128, D])), gpost_sem, writes=[gpost_res])
            sg_flat = sg[:].rearrange("p a n -> p (a n)")
            jstate = {}

            def J0(blk):
                stt, sres = new_st()
                jstate[blk] = (stt, sres)
                P.op("act", (lambda blk, stt: (lambda e: e.activation(out=junk4b.ap, in_=pbig[blk][:, :], func=AF.Square, accum_out=stt[:, 0:1])))(blk, stt),
                     writes=[pr[2 * blk], pr[2 * blk + 1], sres] + list(junk4b.res))

            def J1(blk):
                stt, sres = jstate[blk]
                rstd_from(stt, sres, 1.0 / D)

            def J2(blk, j=j):
                stt, sres = jstate[blk]
                P.op("dve", (lambda blk, stt: (lambda e: e.scalar_tensor_tensor(
                    out=sg_flat, in0=pbig[blk][:, :], scalar=stt[:, 3:4], in1=gpost[:, :], op0=ALU.mult, op1=ALU.mult)))(blk, stt),
                    reads=[sres, gpost_res], writes=[pr[2 * blk], pr[2 * blk + 1]] + list(sg_res))
                P.op("dve", (lambda blk: (lambda e: e.tensor_tensor(out=hb[blk].ap, in0=hb[blk].ap, in1=sg_flat, op=ALU.add)))(blk),
                     reads=list(sg_res), writes=list(hb[blk].res))
                tok = P.dma("sp", (lambda blk, j: (lambda e: e.dma_start(out=out_own[j, blk * 128:(blk + 1) * 128, :], in_=hb[blk].ap)))(blk, j),
                            out_sems[blk], reads=list(hb[blk].res))
                out_toks[tok[0]] = max(out_toks.get(tok[0], 0), tok[1])
            pipes = [(4, [J0, J1, J2], 0)]
            if j + 1 < nown:
                pipes.insert(0, phase_B_pipe(j + 1))
            emit_pipelines(pipes)

        if debug:
            out_toks[dbg_sem.key] = dbg_sem.count
        P.wait_all("sp", list(out_toks.items()))
        P.run()
    nc._dbg_names = list(dbg_out)
    return nc


_NC_CACHE = {}


def _get_program():
    if "nc" not in _NC_CACHE:
        _NC_CACHE["nc"] = build_program()
    return _NC_CACHE["nc"]


def _consts(half):
    idn = np.eye(128, dtype=np.float32)
    jj = np.arange(128)[:, None]
    ss = np.arange(128)[None, :]
    tri_neg = np.where(jj >= ss, -1.0, 0.0).astype(np.float32)
    rest_neg = np.where(jj < ss, -1.0, 0.0).astype(np.float32)
    flag = idn * (1.0 if half == 0 else 0.0)
    u = np.arange(896)[None, :]
    s_ = np.arange(128)[:, None]
    mm = np.where(s_ >= u - 384, NEG, 0.0).astype(np.float32)
    cbf = np.concatenate([idn, tri_neg, rest_neg, flag, mm], axis=1).astype(np.float32)
    ones = np.ones((128, 128), np.float32)
    blockones = np.zeros((128, 128), np.float32)
    blockones[:64, :64] = 1.0
    blockones[64:, 64:] = 1.0
    cf = np.concatenate([ones, blockones], axis=1)
    bb = np.zeros((128, 8), np.float32)
    if half == 0:
        bb[:, 4:8] = NEG
    return cbf, cf, bb


def kernel(x, g_pre_mix, w_in, conv_w, conv_b, conv_ln_g, conv_ln_b, attn_norm_g, w_out, g_post_mix,
           g_pre_ffn, w_gate, w_up, w_down, g_post_ffn):
    f32 = np.float32
    x = np.asarray(x, f32)
    nc = _get_program()

    def pk(v, n):
        return np.ascontiguousarray(np.asarray(v, f32).reshape(n, 128).T)

    gT = np.concatenate([pk(g_pre_mix[0], 8), pk(g_pre_ffn[0], 8)], axis=1)
    gpost = np.ascontiguousarray(np.stack([np.asarray(g_post_mix[0], f32), np.asarray(g_post_ffn[0], f32)], 0))
    cw = np.asarray(conv_w, f32)[0, :, 0, :]
    convwT = cw.T.reshape(4, 128, 31).transpose(1, 0, 2).reshape(128, 124)
    pvec = np.ascontiguousarray(np.concatenate([convwT, pk(conv_b[0], 4), pk(conv_ln_g[0], 4), pk(conv_ln_b[0], 4),
                                                pk(np.asarray(attn_norm_g, f32)[0].reshape(-1), 4)], axis=1).astype(f32))
    shared = {
        "w_in": np.ascontiguousarray(np.asarray(w_in, f32)[0]),
        "w_out": np.ascontiguousarray(np.asarray(w_out, f32)[0]),
        "w_gate": np.ascontiguousarray(np.asarray(w_gate, f32)[0]),
        "w_up": np.ascontiguousarray(np.asarray(w_up, f32)[0]),
        "w_down": np.ascontiguousarray(np.asarray(w_down, f32)[0]),
        "gT": gT, "gpost": gpost, "pvec": pvec,
    }
    in_maps = []
    for c in range(8):
        b, half = c // 2, c % 2
        xb = x[b]
        xc = xb.reshape(NCH, CH, D)
        own = [2 * j + half for j in range(NOWN)]
        x_own = np.ascontiguousarray(xc[own])
        x_halo = np.zeros((NOWN, 32, D), f32)
        for j, ci in enumerate(own):
            if ci > 0:
                x_halo[j] = xb[ci * CH - 32:ci * CH]
        cbf, cf, bb = _consts(half)
        m = dict(shared)
        m.update({"x_all": np.ascontiguousarray(xb), "x_own": x_own, "x_halo": x_halo, "cbf": cbf, "cf": cf, "bb": bb})
        in_maps.append(m)
    if _NC_CACHE.get("debug_hook") is not None:
        return _NC_CACHE["debug_hook"](in_maps)
    res = run_bass_kernel_spmd(nc, in_maps, core_ids=list(range(8)))
    out = np.empty((NB, S, D), f32)
    for c in range(8):
        b, half = c // 2, c % 2
        oc = np.asarray(res.results[c]["out_own"], f32)
        ov = out[b].reshape(NCH, CH, D)
        for j in range(NOWN):
            ov[2 * j + half] = oc[j]
    return out
```
